# Optimizing a Trainium2 kernel written in Bass

```python
import jax, jax.numpy as jnp
from jax import lax
import numpy as np

D_MODEL = 2048
BATCH = 4
SEQ = 4096
DEPTH = 1
DEC_BATCH = 16
DEC_SEQ = 64
PAST_LEN = 4096

CHUNK = 64
N_HEADS_M = 4
HEAD_DIM_M = 256
N_HEADS_R = 4
HEAD_DIM_R = 256
D_M = N_HEADS_M * HEAD_DIM_M
D_R = N_HEADS_R * HEAD_DIM_R
D_MIX = D_M + D_R
D_IN = 4 * D_M + 4 * D_R + 2 * N_HEADS_M
D_FF = ((8 * D_MODEL + 3 * 256 - 1) // (3 * 256)) * 256
ROPE_BASE = 10000.0
EPS = 1e-6

kernel_name = 'hybrid_mlstm_retention_stream_step'

F32 = jnp.float32


def rmsnorm(x, g):
    xf = x.astype(F32)
    y = xf * lax.rsqrt(jnp.mean(xf * xf, axis=-1, keepdims=True) + EPS) * g.astype(F32)
    return y.astype(x.dtype)


def head_norm(h, g, center):
    if center:
        h = h - jnp.mean(h, axis=-1, keepdims=True)
    h = h * lax.rsqrt(jnp.mean(h * h, axis=-1, keepdims=True) + EPS)
    B, H, T, D = h.shape
    return h.transpose(0, 2, 1, 3).reshape(B, T, H * D) * g.astype(F32)


def rope(x, pos):
    half = x.shape[-1] // 2
    freqs = ROPE_BASE ** (-jnp.arange(half, dtype=F32) / half)
    ang = pos[:, None] * freqs[None, :]
    cos, sin = jnp.cos(ang), jnp.sin(ang)
    x1, x2 = x[..., :half], x[..., half:]
    return jnp.concatenate([x1 * cos - x2 * sin, x2 * cos + x1 * sin], axis=-1)


def to_chunks(x, L):
    B, H, T = x.shape[:3]
    return jnp.moveaxis(x.reshape((B, H, T // L, L) + x.shape[3:]), 2, 0)


def from_chunks(x):
    NC, B, H, L, D = x.shape
    return jnp.moveaxis(x, 0, 2).reshape(B, H, NC * L, D)


def mlstm_chunkwise(q, k, v, log_i, log_f, C0, n0, m0):
    T = q.shape[2]
    L = min(CHUNK, T)
    assert T % L == 0
    mask = jnp.tril(jnp.ones((L, L), dtype=bool))

    def step(carry, inp):
        C, n, m = carry
        qc, kc, vc, ic, fc = inp
        b = jnp.cumsum(fc, axis=-1)
        log_d = jnp.where(mask, b[..., :, None] - b[..., None, :] + ic[..., None, :], -jnp.inf)
        log_inter = b + m[..., None]
        m_t = jnp.maximum(log_inter, jnp.max(log_d, axis=-1))
        w = jnp.exp(log_d - m_t[..., None])
        a = jnp.exp(log_inter - m_t)
        s = jnp.einsum('bhtd,bhsd->bhts', qc, kc) * w
        num = a[..., None] * jnp.einsum('bhtd,bhde->bhte', qc, C) + jnp.einsum('bhts,bhse->bhte', s, vc)
        den = a * jnp.einsum('bhtd,bhd->bht', qc, n) + jnp.sum(s, axis=-1)
        h = num / jnp.maximum(jnp.abs(den), jnp.exp(-m_t))[..., None]
        b_last = b[..., -1]
        log_g = b_last[..., None] - b + ic
        m_new = jnp.maximum(b_last + m, jnp.max(log_g, axis=-1))
        g = jnp.exp(log_g - m_new[..., None])
        dec = jnp.exp(b_last + m - m_new)
        kg = kc * g[..., None]
        C_new = dec[..., None, None] * C + jnp.einsum('bhsd,bhse->bhde', kg, vc)
        n_new = dec[..., None] * n + jnp.sum(kg, axis=2)
        return (C_new, n_new, m_new), h

    xs = (to_chunks(q, L), to_chunks(k, L), to_chunks(v, L), to_chunks(log_i, L), to_chunks(log_f, L))
    (C, n, m), h = lax.scan(step, (C0, n0, m0), xs)
    return from_chunks(h), C, n, m


def retention_chunkwise(q, k, v, log_gamma, R0):
    T = q.shape[2]
    L = min(CHUNK, T)
    assert T % L == 0
    pos = jnp.arange(L, dtype=F32)
    diff = pos[:, None] - pos[None, :]
    lg = log_gamma[:, None, None]
    dmask = jnp.where(diff >= 0, jnp.exp(lg * jnp.maximum(diff, 0.0)), 0.0)
    inter = jnp.exp(log_gamma[:, None] * (pos + 1.0))
    kdec = jnp.exp(log_gamma[:, None] * (L - 1.0 - pos))
    cdec = jnp.exp(log_gamma * L)

    def step(R, inp):
        qc, kc, vc = inp
        s = jnp.einsum('bhtd,bhsd->bhts', qc, kc) * dmask
        o = jnp.einsum('bhts,bhse->bhte', s, vc) + inter[None, :, :, None] * jnp.einsum('bhtd,bhde->bhte', qc, R)
        R_new = cdec[None, :, None, None] * R + jnp.einsum('bhsd,bhse->bhde', kc * kdec[None, :, :, None], vc)
        return R_new, o

    R, o = lax.scan(step, R0, (to_chunks(q, L), to_chunks(k, L), to_chunks(v, L)))
    return from_chunks(o), R


def layer(x, pos, C0, n0, m0, R0, g_norm1, w_in, b_gates, g_mlstm_norm, g_ret_norm,
          w_out, g_norm2, w_gate, w_up, w_down):
    B, T, _ = x.shape
    xn = rmsnorm(x, g_norm1)
    proj = xn @ w_in
    cuts = [D_M, 2 * D_M, 3 * D_M, 4 * D_M, 4 * D_M + D_R, 4 * D_M + 2 * D_R,
            4 * D_M + 3 * D_R, 4 * D_M + 4 * D_R]
    mq, mk, mv, mo, rq, rk, rv, rg, gates = jnp.split(proj, cuts, axis=-1)

    def heads(t, nh):
        return t.reshape(B, T, nh, -1).transpose(0, 2, 1, 3).astype(F32)

    gates = gates.astype(F32) + b_gates.astype(F32)
    log_i = gates[..., :N_HEADS_M].transpose(0, 2, 1)
    log_f = jax.nn.log_sigmoid(gates[..., N_HEADS_M:]).transpose(0, 2, 1)
    h_m, C, n, m = mlstm_chunkwise(heads(mq, N_HEADS_M), heads(mk, N_HEADS_M) * HEAD_DIM_M ** -0.5,
                                   heads(mv, N_HEADS_M), log_i, log_f,
                                   C0.astype(F32), n0.astype(F32), m0.astype(F32))
    h_m = head_norm(h_m, g_mlstm_norm, False) * jax.nn.sigmoid(mo.astype(F32))

    log_gamma = jnp.log(1.0 - jnp.exp2(-5.0 - jnp.arange(N_HEADS_R, dtype=F32)))
    q_r = rope(heads(rq, N_HEADS_R), pos)
    k_r = rope(heads(rk, N_HEADS_R), pos) * HEAD_DIM_R ** -0.5
    h_r, R = retention_chunkwise(q_r, k_r, heads(rv, N_HEADS_R), log_gamma, R0.astype(F32))
    h_r = head_norm(h_r, g_ret_norm, True) * jax.nn.silu(rg.astype(F32))

    mix = jnp.concatenate([h_m, h_r], axis=-1).astype(x.dtype)
    x = x + mix @ w_out
    hn = rmsnorm(x, g_norm2)
    x = x + (jax.nn.silu(hn @ w_gate) * (hn @ w_up)) @ w_down
    return x, C, n, m, R


def trunk(x, pos, C0, n0, m0, R0, g_norm1, w_in, b_gates, g_mlstm_norm, g_ret_norm,
          w_out, g_norm2, w_gate, w_up, w_down, g_final):
    Cs, ns, ms, Rs = [], [], [], []
    for l in range(DEPTH):
        x, C, n, m, R = layer(x, pos, C0[l], n0[l], m0[l], R0[l], g_norm1[l], w_in[l], b_gates[l],
                              g_mlstm_norm[l], g_ret_norm[l], w_out[l], g_norm2[l],
                              w_gate[l], w_up[l], w_down[l])
        Cs.append(C); ns.append(n); ms.append(m); Rs.append(R)
    y = rmsnorm(x, g_final)
    return y, jnp.stack(Cs), jnp.stack(ns), jnp.stack(ms), jnp.stack(Rs)


def setup_inputs(seed: int = 0) -> dict:
    key = jax.random.key(seed)
    ks = jax.random.split(key, 20)
    nrm = jax.random.normal
    b_i = 0.1 * nrm(ks[0], (DEPTH, N_HEADS_M), F32)
    b_f = jnp.linspace(3.0, 6.0, N_HEADS_M, dtype=F32)[None, :] + 0.1 * nrm(ks[1], (DEPTH, N_HEADS_M), F32)
    return {
        'x_prompt': nrm(ks[2], (BATCH, SEQ, D_MODEL), F32),
        'x_sample': nrm(ks[3], (DEC_BATCH, DEC_SEQ, D_MODEL), F32),
        'state_mlstm_C': 0.1 * nrm(ks[4], (DEPTH, DEC_BATCH, N_HEADS_M, HEAD_DIM_M, HEAD_DIM_M), F32),
        'state_mlstm_n': 0.1 * nrm(ks[5], (DEPTH, DEC_BATCH, N_HEADS_M, HEAD_DIM_M), F32),
        'state_mlstm_m': nrm(ks[6], (DEPTH, DEC_BATCH, N_HEADS_M), F32),
        'state_ret': 0.1 * nrm(ks[7], (DEPTH, DEC_BATCH, N_HEADS_R, HEAD_DIM_R, HEAD_DIM_R), F32),
        'g_norm1': 1.0 + 0.02 * nrm(ks[8], (DEPTH, D_MODEL), F32),
        'w_in': nrm(ks[9], (DEPTH, D_MODEL, D_IN), F32) * D_MODEL ** -0.5,
        'b_gates': jnp.concatenate([b_i, b_f], axis=-1),
        'g_mlstm_norm': 1.0 + 0.02 * nrm(ks[10], (DEPTH, D_M), F32),
        'g_ret_norm': 1.0 + 0.02 * nrm(ks[11], (DEPTH, D_R), F32),
        'w_out': nrm(ks[12], (DEPTH, D_MIX, D_MODEL), F32) * D_MIX ** -0.5,
        'g_norm2': 1.0 + 0.02 * nrm(ks[13], (DEPTH, D_MODEL), F32),
        'w_gate': nrm(ks[14], (DEPTH, D_MODEL, D_FF), F32) * D_MODEL ** -0.5,
        'w_up': nrm(ks[15], (DEPTH, D_MODEL, D_FF), F32) * D_MODEL ** -0.5,
        'w_down': nrm(ks[16], (DEPTH, D_FF, D_MODEL), F32) * D_FF ** -0.5,
        'g_final': 1.0 + 0.02 * nrm(ks[17], (D_MODEL,), F32),
    }


def reference(x_prompt, x_sample, state_mlstm_C, state_mlstm_n, state_mlstm_m, state_ret,
              g_norm1, w_in, b_gates, g_mlstm_norm, g_ret_norm, w_out, g_norm2,
              w_gate, w_up, w_down, g_final):
    Bp, Tp, _ = x_prompt.shape
    Ts = x_sample.shape[1]
    pos_p = jnp.arange(Tp, dtype=F32)
    pos_s = PAST_LEN + jnp.arange(Ts, dtype=F32)
    C0 = jnp.zeros((DEPTH, Bp, N_HEADS_M, HEAD_DIM_M, HEAD_DIM_M), F32)
    n0 = jnp.zeros((DEPTH, Bp, N_HEADS_M, HEAD_DIM_M), F32)
    m0 = jnp.zeros((DEPTH, Bp, N_HEADS_M), F32)
    R0 = jnp.zeros((DEPTH, Bp, N_HEADS_R, HEAD_DIM_R, HEAD_DIM_R), F32)
    y_prompt, C_p, n_p, m_p, R_p = trunk(x_prompt, pos_p, C0, n0, m0, R0, g_norm1, w_in, b_gates,
                                         g_mlstm_norm, g_ret_norm, w_out, g_norm2,
                                         w_gate, w_up, w_down, g_final)
    y_sample, C_s, n_s, m_s, R_s = trunk(x_sample, pos_s, state_mlstm_C, state_mlstm_n, state_mlstm_m,
                                         state_ret, g_norm1, w_in, b_gates, g_mlstm_norm, g_ret_norm,
                                         w_out, g_norm2, w_gate, w_up, w_down, g_final)
    return (y_prompt, y_sample, C_p, n_p, m_p, R_p, C_s, n_s, m_s, R_s)
```

```python
import math
from contextlib import ExitStack

import numpy as np
import concourse.bass as bass
import concourse.mybir as mybir
from concourse.bass_utils import run_bass_kernel_spmd

F32 = mybir.dt.float32
BF16 = mybir.dt.bfloat16
AF = mybir.ActivationFunctionType
ALU = mybir.AluOpType
AX = mybir.AxisListType

D = 2048
KC = 16
DIN = 8200
DFF = 5632
FC = 44
NH = 4
HD = 256
EPS = 1e-6
PAST_LEN = 4096
LIM = 16000


class Counter:
    def __init__(self, name):
        self.name = name
        self.epoch = 0
        self.cur = 0

    def peek(self, inc):
        if self.cur + inc > LIM:
            return ((self.name, self.epoch + 1), inc)
        return ((self.name, self.epoch), self.cur + inc)

    def next(self, inc):
        if self.cur + inc > LIM:
            self.epoch += 1
            self.cur = 0
        self.cur += inc
        return ((self.name, self.epoch), self.cur)


class Res:
    __slots__ = ("name", "w", "r")

    def __init__(self, name):
        self.name = name
        self.w = None
        self.r = []


ENGS = ["pe", "act", "dve", "pool", "sp"]


class Prog:
    def __init__(self):
        self.ops = {e: [] for e in ENGS}
        self.cnt = {e: Counter("E_" + e) for e in ENGS}
        self.seen = {e: {} for e in ENGS}
        self.semkeys = set()
        self.dcount = 0

    def _waits(self, eng, reads, writes):
        need = {}

        def add(t):
            if t is None:
                return
            k, v = t
            if need.get(k, 0) < v:
                need[k] = v

        for r in reads:
            add(r.w)
        for w in writes:
            add(w.w)
            for t in w.r:
                add(t)
        out = []
        for k, v in need.items():
            if eng == "pe" and k[0] == "E_pe":
                continue
            if self.seen[eng].get(k, 0) >= v:
                continue
            self.seen[eng][k] = v
            out.append((k, v))
        return out

    def _commit(self, tok, reads, writes):
        self.semkeys.add(tok[0])
        for r in reads:
            r.r.append(tok)
        for w in writes:
            w.w = tok
            w.r = []

    def op(self, eng, fn, reads=(), writes=(), sig=True):
        waits = self._waits(eng, reads, writes)
        if sig:
            tok = self.cnt[eng].next(1)
            inc = (tok[0], 1)
        else:
            tok = self.cnt[eng].peek(1)
            inc = None
        self.ops[eng].append((waits, fn, inc))
        self._commit(tok, reads, writes)

    def dma(self, eng, out, in_, reads, writes, semc, **kw):
        waits = self._waits(eng, reads, writes)
        tok = semc.next(16)
        self.ops[eng].append(
            (waits, lambda e: e.dma_start(out=out, in_=in_, **kw), (tok[0], 16)))
        self._commit(tok, reads, writes)
        return tok

    def dma_group(self, eng, items, reads, writes, semc):
        waits = self._waits(eng, reads, writes)
        tok = None
        for i, (out, in_, kw) in enumerate(items):
            tok = semc.next(16)
            self.ops[eng].append(
                (waits if i == 0 else [], (lambda e, out=out, in_=in_, kw=kw: e.dma_start(out=out, in_=in_, **kw)),
                 (tok[0], 16)))
        self._commit(tok, reads, writes)
        return tok

    def wait_all(self, eng, toks):
        waits = []
        for k, v in toks:
            if self.seen[eng].get(k, 0) < v:
                self.seen[eng][k] = v
                waits.append((k, v))
        self.ops[eng].append((waits, None, None))

    def emit(self, nc, st):
        keys = sorted(self.semkeys)
        sems = {}
        for i, k in enumerate(keys):
            sems[k] = st.enter_context(nc.semaphore(f"sm{i}"))
        with nc.Block() as block:
            def mk(name):
                def run(e):
                    for waits, fn, inc in self.ops[name]:
                        for k, v in waits:
                            e.wait_ge(sems[k], v)
                        if fn is None:
                            continue
                        ins = fn(e)
                        if inc is not None:
                            ins.then_inc(sems[inc[0]], inc[1])
                return run
            block.tensor(mk("pe"))
            block.scalar(mk("act"))
            block.vector(mk("dve"))
            block.gpsimd(mk("pool"))
            block.sync(mk("sp"))


def make_plan(TPRE, TP, NS, LS=64):
    tok = 0
    ytok = 0
    pre, main = [], []
    for c in range(TPRE // 128):
        pre.append(dict(seq=0, L=128, tok0=tok, ytok0=None, first=(c == 0), last=False, prefix=True,
                        endpre=(c == TPRE // 128 - 1)))
        tok += 128
    for c in range(TP // 128):
        main.append(dict(seq=0, L=128, tok0=tok, ytok0=ytok, first=(TPRE == 0 and c == 0),
                         last=(c == TP // 128 - 1), prefix=False, endpre=False))
        tok += 128
        ytok += 128
    blocks = [pre[i:i + 4] for i in range(0, len(pre), 4)]
    tail = []
    if NS and NS <= 3 and len(main) >= 2:
        tail = [main[-1]]
        main = main[:-1]
    blocks += [main[i:i + 4] for i in range(0, len(main), 4)]
    sblk = list(tail)
    for sq in range(NS):
        sblk.append(dict(seq=1 + sq, L=LS, tok0=tok, ytok0=ytok, first=True, last=True, prefix=False, endpre=False))
        tok += LS
        ytok += LS
    if sblk:
        blocks.append(sblk)
    for b in blocks:
        off = 0
        for ch in b:
            ch["boff"] = off
            off += ch["L"]
    return blocks, tok, ytok


def build_program(TPRE, TP, NS):
    blocks, NTOK, NYTOK = make_plan(TPRE, TP, NS)
    NSEQ = 1 + NS
    nc = bass.Bass("TRN2", target_bir_lowering=False)
    st = ExitStack()

    def din(name, shape, dt=F32):
        return nc.dram_tensor(name, list(shape), dt, kind="ExternalInput").ap()

    def dout(name, shape, dt=F32):
        return nc.dram_tensor(name, list(shape), dt, kind="ExternalOutput").ap()

    x_d = din("x", [NTOK, D])
    stC_d = din("stC", [NSEQ, NH, HD, HD])
    stn_d = din("stn", [NSEQ, NH, HD])
    stm_d = din("stm", [NSEQ, 128, NH])
    stR_d = din("stR", [NSEQ, NH, HD, HD])
    w_in_d = din("w_in", [D, DIN])
    w_out_d = din("w_out", [D, D])
    w_gate_d = din("w_gate", [D, DFF])
    w_up_d = din("w_up", [D, DFF])
    w_down_d = din("w_down", [DFF, D])
    cf_d = din("cf", [128, 416])
    gall_d = din("gall", [128, 73])
    cos_d = din("cosT", [128, NTOK])
    sin_d = din("sinT", [128, NTOK])

    y_d = dout("y", [NYTOK, D])
    oC_d = dout("oC", [NSEQ, NH, HD, HD])
    on_d = dout("on", [NSEQ, NH, HD])
    om_d = dout("om", [NSEQ, NH])
    oR_d = dout("oR", [NSEQ, NH, HD, HD])

    def sb(name, shape, dt=F32):
        return st.enter_context(nc.sbuf_tensor("s_" + name, list(shape), dt))

    def ps(name, shape, dt=F32):
        return st.enter_context(nc.psum_tensor("p_" + name, list(shape), dt))

    p = Prog()

    xT = sb("xT", [128, KC, 512], F32)
    actA = sb("actA", [128, KC, 512], BF16)
    actB = sb("actB", [128, KC, 512], BF16)
    wbuf = [sb(f"wbuf{i}", [128, 8192], BF16) for i in range(2)]
    gW = sb("gW", [128, KC, 8], BF16)
    stage = [sb(f"stage{i}", [128, 1024], F32) for i in range(2)]
    U = sb("U", [128, 22528], BF16)
    FM = U[:, 0:8192].rearrange("p (c t) -> p c t", t=512)
    TMv = U[:, 8192:8192 + 4 * 1028].rearrange("p (c h e) -> p c h e", c=4, h=4)
    TMo = U[:, 12304:12304 + 4096].rearrange("p (c e) -> p c e", c=4)
    mixtm = U[:, 16400:16400 + 1024]
    actT = U[:, 0:22528].rearrange("p (f t) -> p f t", t=512)
    XS = U[:, 0:16384].bitcast(F32).rearrange("p (c e) -> p c e", c=4)
    cf = sb("cf", [128, 416], F32)
    gall = sb("gall", [128, 73], F32)
    identb = sb("identb", [128, 128], BF16)
    onesb = sb("onesb", [128, 128], BF16)
    ropeC = sb("ropeC", [128, 512], F32)
    ropeS = sb("ropeS", [128, 512], F32)
    S32 = [sb(f"S32_{g}", [128, NH, 2, 257], F32) for g in range(2)]
    Sbf = [sb(f"Sbf_{g}", [128, NH, 2, 257], BF16) for g in range(2)]
    mcur = sb("mcur", [128, 4], F32)
    sqb = sb("sqb", [128, 4, 512], BF16)
    rt = [sb(f"rt{i}", [128, 512], F32) for i in range(4)]
    lnt = rt[0]
    rstd = rt[1]
    gsilu = sb("gsilu", [128, 4, 512], BF16)
    STb = sb("STb", [128, 4, 128], BF16)
    kg = sb("kg", [128, 4, 256], BF16)
    gsb = sb("gsb", [128, 4, 8], F32)
    spb = sb("spb", [128, 4, 4], F32)
    e1b = sb("e1b", [128, 4, 4], F32)
    nbb = sb("nbb", [128, 4, 8], F32)
    zb = sb("zb", [128, 4, 4], F32)
    zmb = sb("zmb", [128, 4, 4], F32)
    zm4 = sb("zm4", [128, 4], F32)
    dg4 = sb("dg4", [128, 4], F32)
    Mb = sb("Mb", [128, 4, 4], F32)
    ab = sb("ab", [128, 4, 4], F32)
    ub = sb("ub", [128, 4, 4], F32)
    dfb = sb("dfb", [128, 4, 4], F32)
    tmp4 = sb("tmp4", [128, 8, 4], F32)
    bnst = sb("bnst", [128, 4, 6], F32)
    bnag = sb("bnag", [128, 4, 2], F32)
    otmp = sb("otmp", [128, 256], F32)
    junk = sb("junk", [128, 256], BF16)
    numsb = sb("numsb", [128, 4, 257], F32)
    mnb = sb("mnb", [128, 4, 4], F32)
    mldb = sb("mldb", [128, 4, 4], F32)
    dg16 = sb("dg16", [128, 16], F32)
    zm16 = sb("zm16", [128, 1], F32)
    tmpg = sb("tmpg", [128, 4, 3, 4], F32)

    identf = cf[:, 0:128]
    causal = cf[:, 128:256]
    onesf = cf[:, 256:384]
    rc = cf[:, 384:408]
    c_eps = cf[:, 408:409]
    c_one = cf[:, 409:410]
    c_nln16 = cf[:, 410:411]
    c_zero = cf[:, 411:412]
    g1T = gall[:, 0:16]
    gmT = gall[:, 16:32]
    g2T = gall[:, 32:48]
    gFT = gall[:, 48:64]
    bgb = gall[:, 64:72]
    pmask = gall[:, 72:73]

    PS = [ps(f"ps{i}", [128, 512], F32) for i in range(7)]
    PSB = ps("psb", [128, 1024], BF16)
    R_PS = [Res(f"ps{i}") for i in range(7)]
    R_PSB = Res("psb")

    R_xT = [Res(f"xT{k}") for k in range(KC)]
    R_actA = [Res(f"actA{k}") for k in range(KC)]
    R_actB = [Res(f"actB{k}") for k in range(KC)]
    R_w = [Res("w0"), Res("w1")]
    R_gW = Res("gW")
    R_stage = [Res("stage0"), Res("stage1")]
    R_FM = [Res(f"FM{i}") for i in range(16)]
    R_TMv = [Res(f"TMv{i}") for i in range(4)]
    R_TMo = [Res(f"TMo{i}") for i in range(4)]
    R_mixtm = Res("mixtm")
    R_actT = [Res(f"actT{i}") for i in range(FC)]
    R_const = Res("const")
    R_rope = Res("rope")
    R_S32 = [[Res(f"S32_{g}_{h}") for h in range(NH)] for g in range(2)]
    R_Sbf = [[Res(f"Sbf_{g}_{h}") for h in range(NH)] for g in range(2)]
    R_m = Res("mcur")
    R_sqb = [Res(f"sqb{i}") for i in range(4)]
    R_rt = [Res(f"rt{i}") for i in range(4)]
    R_lnt = R_rt[0]
    R_rstd = R_rt[1]
    R_gsilu = [Res(f"gsilu{i}") for i in range(4)]
    R_STb = [Res(f"STb{h}") for h in range(4)]
    R_kg = [Res(f"kg{h}") for h in range(4)]
    R_gs = [Res(f"gs{c}") for c in range(4)]
    R_small = Res("small")
    R_otmp = Res("otmp")
    R_junk = Res("junk")
    R_xs = [Res(f"xs{c}") for c in range(4)]
    sem_xs = [Counter(f"D_xs{c}") for c in range(4)]
    R_num = [Res(f"num{h}") for h in range(4)]
    R_bn = [Res(f"bn{h}") for h in range(4)]
    R_den = [Res(f"den{h}") for h in range(4)]
    R_mld = Res("mld")
    sem_mld = Counter("D_mld")

    sem_const = Counter("D_const")
    sem_w = [Counter("D_w0"), Counter("D_w1")]
    sem_stage = [Counter("D_st0"), Counter("D_st1")]
    sem_rope = Counter("D_rope")
    sem_state = Counter("D_state")
    sem_out = Counter("D_out")
    out_toks = []

    U_A = R_FM + R_TMv + R_TMo + [R_mixtm]
    U_B = R_actT

    def mm(out, lhsT, rhs, start, stop, reads, writes, sig):
        p.op("pe", lambda e: e.matmul(out, lhsT, rhs, start=start, stop=stop), reads, writes, sig)

    def tr(out, in_, ident, reads, writes, sig=True):
        p.op("pe", lambda e: e.transpose(out, in_, ident), reads, writes, sig)

    def act(out, in_, func, reads, writes, scale=None, bias=None, accum=None):
        kw = {}
        if scale is not None:
            kw["scale"] = scale
        if bias is not None:
            kw["bias"] = bias
        if accum is not None:
            kw["accum_out"] = accum
        p.op("act", lambda e: e.activation(out=out, in_=in_, func=func, **kw), reads, writes)

    def tt(eng, out, in0, in1, op, reads, writes):
        p.op(eng, lambda e: e.tensor_tensor(out=out, in0=in0, in1=in1, op=op), reads, writes)

    def ts(eng, out, in0, s1, s2, op0, op1, reads, writes):
        if op1 is None:
            p.op(eng, lambda e: e.tensor_scalar(out=out, in0=in0, scalar1=s1, scalar2=None, op0=op0),
                 reads, writes)
        else:
            p.op(eng, lambda e: e.tensor_scalar(out=out, in0=in0, scalar1=s1, scalar2=s2, op0=op0, op1=op1),
                 reads, writes)

    def stt(out, in0, scalar, in1, op0, op1, reads, writes):
        p.op("dve", lambda e: e.scalar_tensor_tensor(out=out, in0=in0, scalar=scalar, in1=in1, op0=op0, op1=op1),
             reads, writes)

    def cp(eng, out, in_, reads, writes):
        if eng == "act":
            act(out, in_, AF.Copy, reads, writes)
        else:
            p.op(eng, lambda e: e.tensor_copy(out=out, in_=in_), reads, writes)

    def scp(eng, out, in_, sc_ap, reads, writes):
        if eng == "act":
            act(out, in_, AF.Copy, reads, writes, scale=sc_ap)
        else:
            ts("dve", out, in_, sc_ap, None, ALU.mult, None, reads, writes)

    evq = [0]

    def ev_eng():
        evq[0] += 1
        return "act" if evq[0] % 2 else "dve"

    psq = [0]

    def next_ps(n=4):
        psq[0] = (psq[0] + 1) % n
        return psq[0]

    p.dma_group("sp", [(cf[:, :], cf_d[:, :], {}), (gall[:, :], gall_d[:, :], {})], [], [R_const], sem_const)
    cp("dve", identb[:, :], identf, [R_const], [R_const])
    cp("dve", onesb[:, :], onesf, [R_const], [R_const])
    p.dma("pool", gW[:, :, :], w_in_d[:, 8192:8200].rearrange("(k p) c -> p k c", p=128),
          [], [R_gW], Counter("D_gw"))

    wslot = [0]

    wsc = nc.dram_tensor("wsc", [58, 128, 8192], BF16).ap()
    slab_ids = {}
    R_sc = {}
    sem_wb = [Counter("D_wb0"), Counter("D_wb1")]

    def load_slab(key, src_ap, n, shape_fn):
        s = wslot[0]
        wslot[0] ^= 1
        if key not in slab_ids:
            idx = len(slab_ids)
            slab_ids[key] = idx
            R_sc[idx] = Res(f"sc{idx}")
            p.dma("pool", shape_fn(wbuf[s][:, 0:n]), src_ap, [], [R_w[s]], sem_w[s])
            p.dma("sp", wsc[idx, :, 0:n], wbuf[s][:, 0:n], [R_w[s]], [R_sc[idx]], sem_wb[s])
        else:
            idx = slab_ids[key]
            p.dma("pool", wbuf[s][:, 0:n], wsc[idx, :, 0:n], [R_sc[idx]], [R_w[s]], sem_w[s])
        return s

    def slab_k16(wd, c0, ncol=512, name=None):
        src = wd[:, c0:c0 + ncol].rearrange("(k p) c -> p k c", p=128)
        fn = lambda a: a.rearrange("p (k c) -> p k c", c=ncol)
        s = load_slab((wd.tensor.name, c0), src, 16 * ncol, fn)
        return s, fn(wbuf[s][:, 0:16 * ncol])

    def slab_down(c0):
        src = w_down_d[:, c0:c0 + 128].rearrange("(k p) c -> p k c", p=128)
        fn = lambda a: a.rearrange("p (k c) -> p k c", c=128)
        s = load_slab(("w_down", c0), src, FC * 128, fn)
        return s, fn(wbuf[s][:, 0:FC * 128])

    sem_state_g = [Counter("D_state0"), Counter("D_state1")]
    sem_out_g = [Counter("D_out0"), Counter("D_out1")]

    def load_state(seq, g):
        items = []
        src = stC_d if g == 0 else stR_d
        for h in range(NH):
            items.append((S32[g][:, h, :, 0:256], src[seq, h].rearrange("(a p) e -> p a e", p=128), {}))
            if g == 0:
                items.append((S32[0][:, h, :, 256:257], stn_d[seq, h].rearrange("(a p o) -> p a o", p=128, o=1),
                              dict(allow_slow_non_contiguous=True)))
        p.dma_group("sp", items, [], R_S32[g], sem_state_g[g])

    def store_state(seq, g, ci=0):
        items = []
        dst = oC_d if g == 0 else oR_d
        for h in range(NH):
            items.append((dst[seq, h].rearrange("(a p) e -> p a e", p=128), S32[g][:, h, :, 0:256], {}))
            if g == 0:
                items.append((on_d[seq, h].rearrange("(a p o) -> p a o", p=128, o=1), S32[0][:, h, :, 256:257],
                              dict(allow_slow_non_contiguous=True)))
        if g == 0:
            items.append((om_d[seq:seq + 1, :], mnb[0:1, ci, :], {}))
        out_toks.append(p.dma_group("sp", items, R_S32[g] + ([R_gs[ci]] if g == 0 else []), [], sem_out_g[g]))

    def mask_state(g):
        for h in range(NH):
            nv = 257 if g == 0 else 256
            ts("dve", S32[g][:, h, :, 0:nv], S32[g][:, h, :, 0:nv], pmask, None, ALU.mult, None,
               [R_S32[g][h], R_const], [R_S32[g][h]])

    def prefetch_x(nblk):
        for ci, ch in enumerate(nblk):
            for r in U_A + U_B:
                if r.w is not None:
                    R_xs[ci].r.append(r.w)
                R_xs[ci].r.extend(r.r)
            p.dma("sp", XS[:ch["L"], ci, :], x_d[ch["tok0"]:ch["tok0"] + ch["L"], :], [], [R_xs[ci]], sem_xs[ci])

    import os
    KSTOP = float(os.environ.get("KSTOP", "0"))

    class _Stop(Exception):
        pass

    def ckpt(n):
        if KSTOP and n >= KSTOP:
            raise _Stop()

    prefetch_x(blocks[0])
    try:
      for bi, blk in enumerate(blocks):
          NB = sum(ch["L"] for ch in blk)
          t0 = blk[0]["tok0"]
          nch = len(blk)

          p.dma_group("sp", [(ropeC[:, 0:NB], cos_d[:, t0:t0 + NB], {}), (ropeS[:, 0:NB], sin_d[:, t0:t0 + NB], {})],
                      [], [R_rope], sem_rope)

          for ci, ch in enumerate(blk):
              L, bo = ch["L"], ch["boff"]
              for hf in range(2):
                  for q in range(2):
                      b = next_ps()
                      for j in range(4):
                          kk = hf * 8 + q * 4 + j
                          tr(PS[b][:, j * 128:j * 128 + L], XS[:L, ci, kk * 128:(kk + 1) * 128],
                             identf[:L, :L], [R_xs[ci], R_const], [R_PS[b]], sig=(j == 3))
                      k0 = hf * 8 + q * 4
                      cp(ev_eng(), xT[:, k0:k0 + 4, bo:bo + L],
                         PS[b][:, :].rearrange("p (k t) -> p k t", t=128)[:, :, 0:L],
                         [R_PS[b]], R_xT[k0:k0 + 4])

          for r in U_A + U_B:
              for rx in R_xs:
                  r.r.extend(rx.r)
          sqq = []

          def sq_acc(k):
              s_ = k % 4
              act(sqb[:, s_, 0:NB], xT[:, k, 0:NB], AF.Square, [R_xT[k]], [R_sqb[s_]])
              mm(PS[4][:, 0:NB], onesb[:, :], sqb[:, s_, 0:NB], k == 0, k == KC - 1,
                 [R_sqb[s_], R_const], [R_PS[4]], sig=True)

          def sq_push(k, lag=2):
              sqq.append(k)
              while len(sqq) > lag:
                  sq_acc(sqq.pop(0))

          def sq_flush():
              while sqq:
                  sq_acc(sqq.pop(0))

          def fm_norm(gT, dst, R_dst, pre=False):
              b = 4 if pre else next_ps()
              for k in range(0 if pre else KC):
                  s = k % 4
                  act(sqb[:, s, 0:NB], xT[:, k, 0:NB], AF.Square, [R_xT[k]], [R_sqb[s]])
                  mm(PS[b][:, 0:NB], onesb[:, :], sqb[:, s, 0:NB], k == 0, k == KC - 1,
                     [R_sqb[s], R_const], [R_PS[b]], sig=True)
              act(lnt[:, 0:NB], PS[b][:, 0:NB], AF.Ln, [R_PS[b], R_const], [R_lnt], scale=1.0 / D, bias=c_eps)
              act(rstd[:, 0:NB], lnt[:, 0:NB], AF.Exp, [R_lnt], [R_rstd], scale=-0.5)
              for k in range(KC):
                  stt(dst[:, k, 0:NB], xT[:, k, 0:NB], gT[:, k:k + 1], rstd[:, 0:NB], ALU.mult, ALU.mult,
                      [R_xT[k], R_rstd, R_const], [R_dst[k]])

          ckpt(1)
          fm_norm(g1T, actA, R_actA)
          ckpt(2)

          for ch in blk:
              pass

          def inproj_fm_slab(c0, cb0, rope):
              s, W = slab_k16(w_in_d, c0)
              if not rope:
                  for j in range(4):
                      b = next_ps()
                      for k in range(KC):
                          mm(PS[b][:, 0:NB], W[:, k, j * 128:(j + 1) * 128], actA[:, k, 0:NB], k == 0, k == KC - 1,
                             [R_w[s], R_actA[k]], [R_PS[b]], sig=(k == KC - 1))
                      cp(ev_eng(), FM[:, cb0 + j, 0:NB], PS[b][:, 0:NB], [R_PS[b]], [R_FM[cb0 + j]])
              else:
                  for pr in range(2):
                      bs = []
                      for hf in range(2):
                          j = pr * 2 + hf
                          b = next_ps()
                          bs.append(b)
                          for k in range(KC):
                              mm(PS[b][:, 0:NB], W[:, k, j * 128:(j + 1) * 128], actA[:, k, 0:NB], k == 0, k == KC - 1,
                                 [R_w[s], R_actA[k]], [R_PS[b]], sig=(k == KC - 1))
                      b1, b2 = bs
                      P1, P2 = PS[b1][:, 0:NB], PS[b2][:, 0:NB]
                      cs_, sn_ = ropeC[:, 0:NB], ropeS[:, 0:NB]
                      tt("dve", rt[0][:, 0:NB], P1, cs_, ALU.mult, [R_PS[b1], R_rope], [R_rt[0]])
                      tt("dve", rt[1][:, 0:NB], P2, sn_, ALU.mult, [R_PS[b2], R_rope], [R_rt[1]])
                      tt("dve", rt[2][:, 0:NB], P2, cs_, ALU.mult, [R_PS[b2], R_rope], [R_rt[2]])
                      tt("dve", rt[3][:, 0:NB], P1, sn_, ALU.mult, [R_PS[b1], R_rope], [R_rt[3]])
                      tt("dve", FM[:, cb0 + pr * 2, 0:NB], rt[0][:, 0:NB], rt[1][:, 0:NB], ALU.subtract,
                         [R_rt[0], R_rt[1]], [R_FM[cb0 + pr * 2]])
                      tt("dve", FM[:, cb0 + pr * 2 + 1, 0:NB], rt[2][:, 0:NB], rt[3][:, 0:NB], ALU.add,
                         [R_rt[2], R_rt[3]], [R_FM[cb0 + pr * 2 + 1]])

          def inproj_tm_slab(c0, kind, hh):
              s, W = slab_k16(w_in_d, c0)
              for ci, ch in enumerate(blk):
                  L, bo = ch["L"], ch["boff"]
                  b = next_ps()
                  for k in range(KC):
                      mm(PS[b][:L, 0:512], actA[:, k, bo:bo + L], W[:, k, 0:512], k == 0, k == KC - 1,
                         [R_w[s], R_actA[k]], [R_PS[b]], sig=(k == KC - 1))
                  if kind == "v":
                      cp(ev_eng(), TMv[:L, ci, hh * 2:hh * 2 + 2, 0:256],
                         PS[b][:L, :].rearrange("p (h e) -> p h e", e=256), [R_PS[b]], [R_TMv[ci]])
                  elif kind == "sig":
                      act(TMo[:L, ci, hh * 512:(hh + 1) * 512], PS[b][:L, 0:512], AF.Sigmoid, [R_PS[b]], [R_TMo[ci]])
                  else:
                      act(TMo[:L, ci, hh * 512:(hh + 1) * 512], PS[b][:L, 0:512], AF.Silu, [R_PS[b]], [R_TMo[ci]])

          def mixer_chunk(g, ci, ch, state_only=False, phase=None):
              PA1 = phase in (None, 'A1')
              PA2 = phase in (None, 'A2')
              PB = phase in (None, 'B')
              L, bo = ch["L"], ch["boff"]
              NV = 257 if g == 0 else 256
              RS32, RSbf = R_S32[g], R_Sbf[g]
              if g == 0:
                  us = [ub[:L, ci, h:h + 1] for h in range(NH)]
                  ug = us
                  a_upd = [ab[:, ci, h:h + 1] for h in range(NH)]
              else:
                  us = [rc[:L, h:h + 1] for h in range(NH)]
                  ugc = 8 if L == 128 else 12
                  cdc = 16 if L == 128 else 20
                  ug = [rc[:L, ugc + h:ugc + h + 1] for h in range(NH)]
                  a_upd = [rc[:, cdc + h:cdc + h + 1] for h in range(NH)]
              Rgs = [R_gs[ci], R_const]
              bS = 4
              for h in range(NH if (not state_only and PA1) else 0):
                  for hf in range(2):
                      cbk = 8 + h * 2 + hf
                      cbq = h * 2 + hf
                      mm(PS[bS][:L, h * 128:h * 128 + L], FM[:, cbk, bo:bo + L], FM[:, cbq, bo:bo + L],
                         hf == 0, hf == 1, [R_FM[cbk], R_FM[cbq]], [R_PS[bS]], sig=(hf == 1))
              for h in range(NH if PA1 else 0):
                  for hf in range(2):
                      cbk = 8 + h * 2 + hf
                      tr(PSB[:L, h * 256 + hf * 128:h * 256 + (hf + 1) * 128], FM[:, cbk, bo:bo + L],
                         identb[:, :], [R_FM[cbk], R_const], [R_PSB], sig=(hf == 1))
              ckpt(3.6)
              for h in range(NH if (not state_only and PA1) else 0):
                  stt(STb[:L, h, 0:L], PS[bS][:L, h * 128:h * 128 + L], us[h], causal[:L, :L], ALU.mult, ALU.mult,
                      [R_PS[bS]] + Rgs, [R_STb[h]])
              for h in range(NH if PA1 else 0):
                  act(kg[:L, h, :], PSB[:L, h * 256:(h + 1) * 256], AF.Copy, [R_PSB] + Rgs, [R_kg[h]], scale=ug[h])
              ckpt(3.7)
              for h in range(NH if (not state_only and PA1) else 0):
                  if g == 0:
                      act(Sbf[g][:, h, :, :], S32[g][:, h, :, :], AF.Copy, [RS32[h]] + Rgs, [RSbf[h]], scale=a_upd[h])
                  else:
                      cp("pool", Sbf[g][:, h, :, 0:256], S32[g][:, h, :, 0:256], [RS32[h]], [RSbf[h]])
              for h in range(NH if (not state_only and PA2) else 0):
                  bn = h
                  mm(PS[bn][:L, 0:NV], STb[:L, h, 0:L], TMv[:L, ci, h, 0:NV], True, False,
                     [R_STb[h], R_TMv[ci]], [R_PS[bn]], sig=False)
                  for hf in range(2):
                      cbq = h * 2 + hf
                      mm(PS[bn][:L, 0:NV], FM[:, cbq, bo:bo + L], Sbf[g][:, h, hf, 0:NV], False, hf == 1,
                         [R_FM[cbq], RSbf[h]], [R_PS[bn]], sig=(hf == 1))
                  cp("act", numsb[:L, h, 0:NV], PS[bn][:L, 0:NV], [R_PS[bn]], [R_num[h]])
              ckpt(3.8)
              for h in range(NH if PA1 else 0):
                  for hf in range(2):
                      bu = [5, 6, 0, 1, 2, 3, 5, 6][h * 2 + hf]
                      mm(PS[bu][:, 0:NV], kg[:L, h, hf * 128:(hf + 1) * 128], TMv[:L, ci, h, 0:NV], True, True,
                         [R_kg[h], R_TMv[ci]], [R_PS[bu]], sig=True)
                      stt(S32[g][:, h, hf, 0:NV], S32[g][:, h, hf, 0:NV], a_upd[h], PS[bu][:, 0:NV],
                          ALU.mult, ALU.add, [R_PS[bu], RS32[h]] + Rgs, [RS32[h]])
              ckpt(3.9)
              if state_only or not PB:
                  return
              if g == 0:
                  den = tmp4[:L, 0, :]
                  rr = tmp4[:L, 1, :]
                  ssq = tmp4[:L, 2, :]
                  t1 = tmp4[:L, 3, :]
                  lnv = tmp4[:L, 4, :]
                  rs = tmp4[:L, 5, :]
                  sc = tmp4[:L, 6, :]
                  for h in range(NH):
                      act(tmp4[:L, 0, h:h + 1], numsb[:L, h, 256:257], AF.Abs, [R_num[h]], [R_den[h]])
                      p.op("dve", lambda e, h=h: e.bn_stats(out=bnst[:L, h, :], in_=numsb[:L, h, 0:256]),
                           [R_num[h]], [R_bn[h]])
                  for h in range(NH):
                      p.op("dve", lambda e, h=h: e.bn_aggr(out=bnag[:L, h, :], in_=bnst[:L, h, :]),
                           [R_bn[h]], [R_bn[h]])
                  tt("dve", den, den, dfb[:L, ci, :], ALU.max, R_den + R_bn + [R_small] + Rgs, R_den + [R_small])
                  p.op("dve", lambda e: e.reciprocal(out=rr, in_=den), [R_small], [R_small])
                  tt("dve", ssq, bnag[:L, :, 0], bnag[:L, :, 0], ALU.mult, R_bn + [R_small], [R_small])
                  tt("dve", ssq, ssq, bnag[:L, :, 1], ALU.add, R_bn + [R_small], [R_small])
                  tt("dve", t1, ssq, rr, ALU.mult, [R_small], [R_small])
                  tt("dve", t1, t1, rr, ALU.mult, [R_small], [R_small])
                  act(lnv, t1, AF.Ln, [R_small, R_const], [R_small], bias=c_eps[:L, :])
                  act(rs, lnv, AF.Exp, [R_small], [R_small], scale=-0.5)
                  tt("dve", sc, rs, rr, ALU.mult, [R_small], [R_small])
                  for h in range(NH):
                      stt(mixtm[:L, h * 256:(h + 1) * 256], numsb[:L, h, 0:256], tmp4[:L, 6, h:h + 1],
                          TMo[:L, ci, h * 256:(h + 1) * 256], ALU.mult, ALU.mult,
                          [R_num[h], R_small, R_TMo[ci]], [R_mixtm])
              else:
                  for h in range(NH):
                      p.op("dve", lambda e, h=h: e.bn_stats(out=bnst[:L, h, :], in_=numsb[:L, h, 0:256]),
                           [R_num[h]], [R_bn[h]])
                  for h in range(NH):
                      p.op("dve", lambda e, h=h: e.bn_aggr(out=bnag[:L, h, :], in_=bnst[:L, h, :]),
                           [R_bn[h]], [R_bn[h]])
                  inter = rc[:L, 4:8]
                  var = bnag[:L, :, 1]
                  t1 = tmp4[:L, 3, :]
                  lnv = tmp4[:L, 4, :]
                  rs = tmp4[:L, 5, :]
                  sc = tmp4[:L, 6, :]
                  tt("dve", t1, var, inter, ALU.mult, R_bn + [R_small, R_const], R_bn + [R_small])
                  tt("dve", t1, t1, inter, ALU.mult, [R_small, R_const], [R_small])
                  act(lnv, t1, AF.Ln, [R_small, R_const], [R_small], bias=c_eps[:L, :])
                  act(rs, lnv, AF.Exp, [R_small], [R_small], scale=-0.5)
                  tt("dve", sc, rs, inter, ALU.mult, [R_small, R_const], [R_small])
                  for h in range(NH):
                      stt(otmp[:L, :], numsb[:L, h, 0:256], bnag[:L, h, 0:1], TMo[:L, ci, h * 256:(h + 1) * 256],
                          ALU.subtract, ALU.mult, [R_num[h], R_bn[h], R_small, R_TMo[ci]], [R_otmp])
                      act(mixtm[:L, h * 256:(h + 1) * 256], otmp[:L, :], AF.Copy, [R_otmp, R_small], [R_mixtm],
                          scale=tmp4[:L, 6, h:h + 1])
              ckpt(3.95)
              for f in range(8):
                  tr(PSB[:, f * 128:f * 128 + L], mixtm[:L, f * 128:(f + 1) * 128], identb[:L, :L],
                     [R_mixtm, R_const], [R_PSB], sig=(f == 7))
              ckpt(3.97)
              for f in range(8):
                  kf = g * 8 + f
                  scp("act", actB[:, kf, bo:bo + L], PSB[:, f * 128:f * 128 + L], gmT[:, kf:kf + 1],
                      [R_PSB, R_const], [R_actB[kf]])
              ckpt(3.98)

          PRE = blk[0]["prefix"]
          b = 4
          fl = lambda t: t.rearrange("p c h -> p (c h)")
          groups = []
          for ci, ch in enumerate(blk):
              if groups and blk[groups[-1][0]]["L"] == ch["L"]:
                  groups[-1][1] = ci + 1
              else:
                  groups.append([ci, ci + 1])
          gstages = []
          for c0, c1 in groups:
              def mk_stages(c0=c0, c1=c1):
                  L = blk[c0]["L"]
                  ng = c1 - c0
                  n4 = ng * 4
                  Rg = R_gs[c0:c1]

                  def g1():
                      for ci in range(c0, c1):
                          bo = blk[ci]["boff"]
                          for k in range(KC):
                              mm(PS[b][:L, ci * 8:(ci + 1) * 8], actA[:, k, bo:bo + L], gW[:, k, :], k == 0, k == KC - 1,
                                 [R_gW, R_actA[k]], [R_PS[b]], sig=(k == KC - 1))
                      for ci in range(c0, c1):
                          tt("dve", gsb[:L, ci, :], PS[b][:L, ci * 8:(ci + 1) * 8], bgb[:L, :], ALU.add,
                             [R_PS[b], R_const], [R_gs[ci]])
                      act(e1b[:L, c0:c1, :], gsb[:L, c0:c1, 4:8], AF.Exp, Rg, Rg, scale=-1.0)
                      act(spb[:L, c0:c1, :], e1b[:L, c0:c1, :], AF.Ln, Rg + [R_const], Rg, bias=c_one[:L, :])

                  def g2():
                      mm(PS[b][:L, 64:64 + n4], causal[:L, :L], fl(spb[:L, c0:c1, :]), True, True, Rg + [R_const], [R_PS[b]], True)
                      mm(PS[b][:, 128:128 + n4], onesf[:L, :], fl(spb[:L, c0:c1, :]), True, True, Rg + [R_const], [R_PS[b]], True)
                      tt("dve", zb[:L, c0:c1, :], gsb[:L, c0:c1, 0:4],
                         PS[b][:L, 64:64 + n4].rearrange("p (c h) -> p c h", h=4), ALU.add, [R_PS[b]] + Rg, Rg)
                      cp("act", nbb[:L, c0:c1, 0:4], PS[b][:L, 64:64 + n4].rearrange("p (c h) -> p c h", h=4), [R_PS[b]], Rg)
                      cp("act", nbb[:, c0:c1, 4:8], PS[b][:, 128:128 + n4].rearrange("p (c h) -> p c h", h=4), [R_PS[b]], Rg)

                  def g3():
                      tr(PS[b][0:n4, 256:256 + L], fl(zb[:L, c0:c1, :]), identf[:L, :L], Rg + [R_const], [R_PS[b]])
                      p.op("dve", lambda e, b=b, L=L, n4=n4: e.tensor_reduce(out=zm16[0:n4, 0:1],
                                                                              in_=PS[b][0:n4, 256:256 + L],
                                                                              axis=AX.X, op=ALU.max),
                           [R_PS[b]], [R_small])
                      ts("dve", dg16[0:n4, 0:n4], identf[0:n4, 0:n4], zm16[0:n4, 0:1], None, ALU.mult, None,
                         [R_small, R_const], [R_small])

                  def g4():
                      mm(PS[b][:, 384:384 + n4], onesf[0:n4, :], dg16[0:n4, 0:n4], True, True, [R_small, R_const], [R_PS[b]], True)
                      cp("act", zmb[:, c0:c1, :], PS[b][:, 384:384 + n4].rearrange("p (c h) -> p c h", h=4), [R_PS[b]], Rg)
                  return [g1, g2, g3, g4]
              gstages += mk_stages()

          def g5():
              firsts = [(ci, ch) for ci, ch in enumerate(blk) if ch["first"]]
              if firsts:
                  p.dma_group("sp", [(mldb[:, ci, :], stm_d[ch["seq"]], {}) for ci, ch in firsts], [], [R_mld], sem_mld)
              for ci, ch in enumerate(blk):
                  if ch["first"]:
                      mprev, Rm = mldb[:, ci, :], R_mld
                  elif ci == 0:
                      mprev, Rm = mcur[:, :], R_m
                  else:
                      mprev, Rm = mnb[:, ci - 1, :], R_gs[ci - 1]
                  G = R_gs[ci]
                  L = ch["L"]
                  tt("dve", Mb[:, ci, :], mprev, zmb[:, ci, :], ALU.max, [Rm, G], [G])
                  tt("dve", tmpg[:, ci, 0, :], mprev, Mb[:, ci, :], ALU.subtract, [Rm, G], [G])
                  act(ab[:, ci, :], tmpg[:, ci, 0, :], AF.Exp, [G], [G])
                  tt("dve", tmpg[:L, ci, 1, :], zb[:L, ci, :], Mb[:L, ci, :], ALU.subtract, [G], [G])
                  act(ub[:L, ci, :], tmpg[:L, ci, 1, :], AF.Exp, [G, R_const], [G], bias=c_nln16[:L, :])
                  tt("dve", tmpg[:L, ci, 2, :], nbb[:L, ci, 0:4], Mb[:L, ci, :], ALU.subtract, [G], [G])
                  act(dfb[:L, ci, :], tmpg[:L, ci, 2, :], AF.Exp, [G], [G])
                  tt("dve", mnb[:, ci, :], Mb[:, ci, :], nbb[:, ci, 4:8], ALU.subtract, [G], [G])
              if blk[-1]["endpre"]:
                  ts("dve", mcur[:, :], mnb[:, nch - 1, :], pmask, None, ALU.mult, None, [R_gs[nch - 1], R_const], [R_m])
              else:
                  cp("dve", mcur[:, :], mnb[:, nch - 1, :], [R_gs[nch - 1]], [R_m])
          gstages.append(g5)

          def gstep():
              if gstages:
                  gstages.pop(0)()

          gstep()
          if not PRE:
              inproj_fm_slab(0, 0, False)
              gstep()
              inproj_fm_slab(512, 4, False)
              gstep()
          inproj_fm_slab(1024, 8, False)
          gstep()
          inproj_fm_slab(1536, 12, False)
          gstep()
          for c in range(nch):
              p.op("dve", lambda e, c=c: e.memset(TMv[:, c, :, 256:257], 1.0), [], [R_TMv[c]])
          inproj_tm_slab(2048, "v", 0)
          gstep()
          inproj_tm_slab(2560, "v", 1)
          gstep()
          if not PRE:
              inproj_tm_slab(3072, "sig", 0)
              gstep()
              inproj_tm_slab(3584, "sig", 1)
          while gstages:
              gstep()
          ckpt(3)
          ckpt(3.5)
          for ci, ch in enumerate(blk):
              if ch["first"]:
                  load_state(ch["seq"], 0)
              mixer_chunk(0, ci, ch, state_only=PRE, phase="A1")
              if ch["endpre"]:
                  assert ci == nch - 1
                  mask_state(0)
              if ch["last"]:
                  store_state(ch["seq"], 0, ci)
              if not PRE:
                  if ci > 0:
                      mixer_chunk(0, ci - 1, blk[ci - 1], phase="B")
                  mixer_chunk(0, ci, ch, phase="A2")
          if not PRE:
              inproj_fm_slab(4096, 0, True)
              mixer_chunk(0, nch - 1, blk[-1], phase="B")

          ckpt(4)
          if not PRE:
              inproj_fm_slab(4608, 4, True)
          inproj_fm_slab(5120, 8, True)
          inproj_fm_slab(5632, 12, True)
          inproj_tm_slab(6144, "v", 0)
          inproj_tm_slab(6656, "v", 1)
          if not PRE:
              inproj_tm_slab(7168, "silu", 0)
              inproj_tm_slab(7680, "silu", 1)
          for ci, ch in enumerate(blk):
              if ch["first"]:
                  load_state(ch["seq"], 1)
              mixer_chunk(1, ci, ch, state_only=PRE, phase="A1")
              if ch["endpre"]:
                  mask_state(1)
              if ch["last"]:
                  store_state(ch["seq"], 1)
              if not PRE:
                  if ci > 0:
                      mixer_chunk(1, ci - 1, blk[ci - 1], phase="B")
                  mixer_chunk(1, ci, ch, phase="A2")
          if not PRE:
              s0, W0 = slab_k16(w_out_d, 0)
              for j in range(4):
                  for k in range(8):
                      mm(PS[j][:, 0:NB], W0[:, k, j * 128:(j + 1) * 128], actB[:, k, 0:NB], k == 0, False,
                         [R_w[s0], R_actB[k]], [R_PS[j]], sig=False)
              mixer_chunk(1, nch - 1, blk[-1], phase="B")
              for j in range(4):
                  for k in range(8, KC):
                      mm(PS[j][:, 0:NB], W0[:, k, j * 128:(j + 1) * 128], actB[:, k, 0:NB], False, k == KC - 1,
                         [R_w[s0], R_actB[k]], [R_PS[j]], sig=(k == KC - 1))
                  tt("dve", xT[:, j, 0:NB], xT[:, j, 0:NB], PS[j][:, 0:NB], ALU.add, [R_PS[j], R_xT[j]], [R_xT[j]])
                  sq_push(j)
          if PRE:
              if bi + 1 < len(blocks):
                  prefetch_x(blocks[bi + 1])
              continue

          ckpt(5)
          for sl in range(1, 4):
              s, W = slab_k16(w_out_d, sl * 512)
              for j in range(4):
                  cb = sl * 4 + j
                  b = next_ps()
                  for k in range(KC):
                      mm(PS[b][:, 0:NB], W[:, k, j * 128:(j + 1) * 128], actB[:, k, 0:NB], k == 0, k == KC - 1,
                         [R_w[s], R_actB[k]], [R_PS[b]], sig=(k == KC - 1))
                  tt("dve", xT[:, cb, 0:NB], xT[:, cb, 0:NB], PS[b][:, 0:NB], ALU.add, [R_PS[b], R_xT[cb]], [R_xT[cb]])
                  sq_push(cb)

          ckpt(6)
          sq_flush()
          fm_norm(g2T, actA, R_actA, pre=True)

          for i in range(11):
              s, W = slab_k16(w_gate_d, i * 512)
              for j in range(4):
                  b = next_ps()
                  for k in range(KC):
                      mm(PS[b][:, 0:NB], W[:, k, j * 128:(j + 1) * 128], actA[:, k, 0:NB], k == 0, k == KC - 1,
                         [R_w[s], R_actA[k]], [R_PS[b]], sig=(k == KC - 1))
                  act(gsilu[:, j, 0:NB], PS[b][:, 0:NB], AF.Silu, [R_PS[b]], [R_gsilu[j]])
              s, W = slab_k16(w_up_d, i * 512)
              for j in range(4):
                  f = i * 4 + j
                  b = next_ps()
                  for k in range(KC):
                      mm(PS[b][:, 0:NB], W[:, k, j * 128:(j + 1) * 128], actA[:, k, 0:NB], k == 0, k == KC - 1,
                         [R_w[s], R_actA[k]], [R_PS[b]], sig=(k == KC - 1))
                  tt("dve", actT[:, f, 0:NB], PS[b][:, 0:NB], gsilu[:, j, 0:NB], ALU.mult,
                     [R_PS[b], R_gsilu[j]], [R_actT[f]] + (U_A if f == 0 else []))

          ckpt(7)
          for cb in range(KC):
              s, W = slab_down(cb * 128)
              b = next_ps()
              for f in range(FC):
                  mm(PS[b][:, 0:NB], W[:, f, :], actT[:, f, 0:NB], f == 0, f == FC - 1,
                     [R_w[s], R_actT[f]], [R_PS[b]], sig=(f == FC - 1))
              tt("dve", xT[:, cb, 0:NB], xT[:, cb, 0:NB], PS[b][:, 0:NB], ALU.add, [R_PS[b], R_xT[cb]], [R_xT[cb]])
              sq_push(cb)
          sq_flush()

          if bi + 1 < len(blocks):
              prefetch_x(blocks[bi + 1])
          ckpt(8)
          b = 4
          act(lnt[:, 0:NB], PS[b][:, 0:NB], AF.Ln, [R_PS[b], R_const], [R_lnt], scale=1.0 / D, bias=c_eps)
          act(rstd[:, 0:NB], lnt[:, 0:NB], AF.Exp, [R_lnt], [R_rstd], scale=-0.5)
          for k in range(KC):
              stt(xT[:, k, 0:NB], xT[:, k, 0:NB], gFT[:, k:k + 1], rstd[:, 0:NB], ALU.mult, ALU.mult,
                  [R_xT[k], R_rstd, R_const], [R_xT[k]])
          for ci, ch in enumerate(blk):
              L, bo = ch["L"], ch["boff"]
              for hf in range(2):
                  sg = (2 * ci + hf) % 2
                  for q in range(2):
                      b = next_ps()
                      for j in range(4):
                          kk = hf * 8 + q * 4 + j
                          tr(PS[b][:L, j * 128:(j + 1) * 128], xT[:, kk, bo:bo + L], identf[:, :],
                             [R_xT[kk], R_const], [R_PS[b]], sig=(j == 3))
                      cp(ev_eng(), stage[sg][:L, q * 512:(q + 1) * 512], PS[b][:L, 0:512], [R_PS[b]], [R_stage[sg]])
                  out_toks.append(p.dma("sp", y_d[ch["ytok0"]:ch["ytok0"] + L, hf * 1024:(hf + 1) * 1024],
                                        stage[sg][:L, :], [R_stage[sg]], [], sem_stage[sg]))

          for r in U_A:
              for rb in U_B:
                  r.r.extend(rb.r)

    except _Stop:
        pass
    last = {}
    for k, v in out_toks:
        last[k] = max(last.get(k, 0), v)
    p.wait_all("sp", list(last.items()))
    p.emit(nc, st)
    st.close()
    return nc, NTOK, NYTOK, NSEQ


_CACHE = {}


def _consts(pos):
    cf = np.zeros((128, 416), np.float32)
    cf[:, 0:128] = np.eye(128, dtype=np.float32)
    s = np.arange(128)
    cf[:, 128:256] = (s[:, None] <= s[None, :]).astype(np.float32)
    cf[:, 256:384] = 1.0
    lg = np.log(1.0 - np.exp2(-5.0 - np.arange(4, dtype=np.float64)))
    sc = 1.0 / 16.0
    cf[:, 384:388] = np.exp(-lg[None, :] * (s[:, None] + 1.0)) * sc
    cf[:, 388:392] = np.exp(lg[None, :] * (s[:, None] + 1.0))
    cf[:, 392:396] = np.exp(lg[None, :] * (127.0 - s[:, None])) * sc
    cf[:, 396:400] = np.exp(lg[None, :] * (63.0 - s[:, None])) * sc
    cf[:, 400:404] = np.exp(lg[None, :] * 128.0)
    cf[:, 404:408] = np.exp(lg[None, :] * 64.0)
    cf[:, 408] = EPS
    cf[:, 409] = 1.0
    cf[:, 410] = -math.log(16.0)
    pos = np.asarray(pos, np.float64)
    freqs = 10000.0 ** (-np.arange(128, dtype=np.float64) / 128)
    ang = pos[None, :] * freqs[:, None]
    return cf, np.cos(ang).astype(np.float32), np.sin(ang).astype(np.float32)


def core_positions(TPRE, TP, NS, half):
    return np.concatenate([np.arange(TPRE, dtype=np.float32), half * TP + np.arange(TP, dtype=np.float32)] +
                          [PAST_LEN + np.arange(64, dtype=np.float32) for _ in range(NS)])


def make_gall(g_norm1, g_mlstm_norm, g_ret_norm, g_norm2, g_final, b_gates, maskval):
    f = lambda a: np.asarray(a, dtype=np.float32)
    gm = np.concatenate([f(g_mlstm_norm).reshape(-1), f(g_ret_norm).reshape(-1)])
    fm = lambda v: np.ascontiguousarray(f(v).reshape(16, 128).T)
    gall = np.concatenate([fm(g_norm1), fm(gm), fm(g_norm2), fm(g_final),
                           np.broadcast_to(f(b_gates).reshape(1, 8), (128, 8)),
                           np.full((128, 1), maskval, np.float32)], axis=1)
    return np.ascontiguousarray(gall, dtype=np.float32)


_CACHE = {}


def kernel(x_prompt, x_sample, state_mlstm_C, state_mlstm_n, state_mlstm_m, state_ret,
           g_norm1, w_in, b_gates, g_mlstm_norm, g_ret_norm, w_out, g_norm2,
           w_gate, w_up, w_down, g_final):
    TPRE, TP, NS = 2048, 2048, 2
    f = lambda a: np.ascontiguousarray(np.asarray(a, dtype=np.float32))
    x_prompt, x_sample = f(x_prompt), f(x_sample)
    shared = dict(w_in=f(w_in)[0], w_out=f(w_out)[0], w_gate=f(w_gate)[0], w_up=f(w_up)[0], w_down=f(w_down)[0])
    sC, sn, sm, sR = f(state_mlstm_C)[0], f(state_mlstm_n)[0], f(state_mlstm_m)[0], f(state_ret)[0]
    per_core = []
    for c in range(8):
        pb, half = c // 2, c % 2
        ss = [2 * c, 2 * c + 1]
        cf, cosT, sinT = _consts(core_positions(TPRE, TP, NS, half))
        gall = make_gall(g_norm1, g_mlstm_norm, g_ret_norm, g_norm2, g_final, b_gates, float(half))
        x = np.concatenate([x_prompt[pb, 0:TPRE], x_prompt[pb, half * TP:(half + 1) * TP]] +
                           [x_sample[s] for s in ss], axis=0)
        zC = np.zeros((1, NH, HD, HD), np.float32)
        stC = np.concatenate([zC, sC[ss]], axis=0)
        stR = np.concatenate([zC, sR[ss]], axis=0)
        stn = np.concatenate([np.zeros((1, NH, HD), np.float32), sn[ss]], axis=0)
        stm = np.concatenate([np.zeros((1, NH), np.float32), sm[ss]], axis=0)
        stm = np.broadcast_to(stm[:, None, :], (stm.shape[0], 128, NH))
        d = dict(shared)
        d.update(cf=cf, gall=gall, cosT=cosT, sinT=sinT,
                 x=np.ascontiguousarray(x), stC=np.ascontiguousarray(stC), stn=np.ascontiguousarray(stn),
                 stm=np.ascontiguousarray(stm), stR=np.ascontiguousarray(stR))
        per_core.append(d)
    key = (TPRE, TP, NS)
    if key not in _CACHE:
        _CACHE[key] = build_program(TPRE, TP, NS)
    nc = _CACHE[key][0]
    res = run_bass_kernel_spmd(nc, per_core, core_ids=list(range(8))).results
    y_p = np.stack([np.concatenate([res[2 * b]["y"][:TP], res[2 * b + 1]["y"][:TP]], axis=0) for b in range(4)], axis=0)
    y_s = np.stack([res[s // 2]["y"][TP + (s % 2) * 64: TP + (s % 2 + 1) * 64] for s in range(16)], axis=0)
    C_p = np.stack([res[2 * b + 1]["oC"][0] for b in range(4)], axis=0)[None]
    n_p = np.stack([res[2 * b + 1]["on"][0] for b in range(4)], axis=0)[None]
    m_p = np.stack([res[2 * b + 1]["om"][0] for b in range(4)], axis=0)[None]
    R_p = np.stack([res[2 * b + 1]["oR"][0] for b in range(4)], axis=0)[None]
    C_s = np.stack([res[s // 2]["oC"][1 + s % 2] for s in range(16)], axis=0)[None]
    n_s = np.stack([res[s // 2]["on"][1 + s % 2] for s in range(16)], axis=0)[None]
    m_s = np.stack([res[s // 2]["om"][1 + s % 2] for s in range(16)], axis=0)[None]
    R_s = np.stack([res[s // 2]["oR"][1 + s % 2] for s in range(16)], axis=0)[None]
    return (y_p, y_s, C_p, n_p, m_p, R_p, C_s, n_s, m_s, R_s)
```

```python
import math
from contextlib import ExitStack

import numpy as np
import concourse.bass as bass
import concourse.mybir as mybir
from concourse.bass_utils import run_bass_kernel_spmd

F32 = mybir.dt.float32
BF16 = mybir.dt.bfloat16
AF = mybir.ActivationFunctionType
ALU = mybir.AluOpType
AX = mybir.AxisListType

D = 2048
KC = 16
DIN = 8200
DFF = 5632
FC = 44
NH = 4
HD = 256
EPS = 1e-6
PAST_LEN = 4096
LIM = 16000


class Counter:
    def __init__(self, name):
        self.name = name
        self.epoch = 0
        self.cur = 0

    def peek(self, inc):
        if self.cur + inc > LIM:
            return ((self.name, self.epoch + 1), inc)
        return ((self.name, self.epoch), self.cur + inc)

    def next(self, inc):
        if self.cur + inc > LIM:
            self.epoch += 1
            self.cur = 0
        self.cur += inc
        return ((self.name, self.epoch), self.cur)


class Res:
    __slots__ = ("name", "w", "r")

    def __init__(self, name):
        self.name = name
        self.w = None
        self.r = []


ENGS = ["pe", "act", "dve", "pool", "sp"]


class Prog:
    def __init__(self):
        self.ops = {e: [] for e in ENGS}
        self.cnt = {e: Counter("E_" + e) for e in ENGS}
        self.seen = {e: {} for e in ENGS}
        self.semkeys = set()
        self.dcount = 0

    def _waits(self, eng, reads, writes):
        need = {}

        def add(t):
            if t is None:
                return
            k, v = t
            if need.get(k, 0) < v:
                need[k] = v

        for r in reads:
            add(r.w)
        for w in writes:
            add(w.w)
            for t in w.r:
                add(t)
        out = []
        for k, v in need.items():
            if eng == "pe" and k[0] == "E_pe":
                continue
            if self.seen[eng].get(k, 0) >= v:
                continue
            self.seen[eng][k] = v
            out.append((k, v))
        return out

    def _commit(self, tok, reads, writes):
        self.semkeys.add(tok[0])
        for r in reads:
            r.r.append(tok)
        for w in writes:
            w.w = tok
            w.r = []

    def op(self, eng, fn, reads=(), writes=(), sig=True):
        waits = self._waits(eng, reads, writes)
        if sig:
            tok = self.cnt[eng].next(1)
            inc = (tok[0], 1)
        else:
            tok = self.cnt[eng].peek(1)
            inc = None
        self.ops[eng].append((waits, fn, inc))
        self._commit(tok, reads, writes)

    def dma(self, eng, out, in_, reads, writes, semc, **kw):
        waits = self._waits(eng, reads, writes)
        tok = semc.next(16)
        self.ops[eng].append(
            (waits, lambda e: e.dma_start(out=out, in_=in_, **kw), (tok[0], 16)))
        self._commit(tok, reads, writes)
        return tok

    def dma_group(self, eng, items, reads, writes, semc):
        waits = self._waits(eng, reads, writes)
        tok = None
        for i, (out, in_, kw) in enumerate(items):
            tok = semc.next(16)
            self.ops[eng].append(
                (waits if i == 0 else [], (lambda e, out=out, in_=in_, kw=kw: e.dma_start(out=out, in_=in_, **kw)),
                 (tok[0], 16)))
        self._commit(tok, reads, writes)
        return tok

    def wait_all(self, eng, toks):
        waits = []
        for k, v in toks:
            if self.seen[eng].get(k, 0) < v:
                self.seen[eng][k] = v
                waits.append((k, v))
        self.ops[eng].append((waits, None, None))

    def emit(self, nc, st):
        keys = sorted(self.semkeys)
        sems = {}
        for i, k in enumerate(keys):
            sems[k] = st.enter_context(nc.semaphore(f"sm{i}"))
        with nc.Block() as block:
            def mk(name):
                def run(e):
                    for waits, fn, inc in self.ops[name]:
                        for k, v in waits:
                            e.wait_ge(sems[k], v)
                        if fn is None:
                            continue
                        ins = fn(e)
                        if inc is not None:
                            ins.then_inc(sems[inc[0]], inc[1])
                return run
            block.tensor(mk("pe"))
            block.scalar(mk("act"))
            block.vector(mk("dve"))
            block.gpsimd(mk("pool"))
            block.sync(mk("sp"))


def make_plan(TPRE, TP, NS, LS=64):
    tok = 0
    ytok = 0
    pre, main = [], []
    for c in range(TPRE // 128):
        pre.append(dict(seq=0, L=128, tok0=tok, ytok0=None, first=(c == 0), last=False, prefix=True,
                        endpre=(c == TPRE // 128 - 1)))
        tok += 128
    for c in range(TP // 128):
        main.append(dict(seq=0, L=128, tok0=tok, ytok0=ytok, first=(TPRE == 0 and c == 0),
                         last=(c == TP // 128 - 1), prefix=False, endpre=False))
        tok += 128
        ytok += 128
    blocks = [pre[i:i + 4] for i in range(0, len(pre), 4)]
    tail = []
    if NS and NS <= 3 and len(main) >= 2:
        tail = [main[-1]]
        main = main[:-1]
    blocks += [main[i:i + 4] for i in range(0, len(main), 4)]
    sblk = list(tail)
    for sq in range(NS):
        sblk.append(dict(seq=1 + sq, L=LS, tok0=tok, ytok0=ytok, first=True, last=True, prefix=False, endpre=False))
        tok += LS
        ytok += LS
    if sblk:
        blocks.append(sblk)
    for b in blocks:
        off = 0
        for ch in b:
            ch["boff"] = off
            off += ch["L"]
    return blocks, tok, ytok


def build_program(TPRE, TP, NS):
    blocks, NTOK, NYTOK = make_plan(TPRE, TP, NS)
    NSEQ = 1 + NS
    nc = bass.Bass("TRN2", target_bir_lowering=False)
    st = ExitStack()

    def din(name, shape, dt=F32):
        return nc.dram_tensor(name, list(shape), dt, kind="ExternalInput").ap()

    def dout(name, shape, dt=F32):
        return nc.dram_tensor(name, list(shape), dt, kind="ExternalOutput").ap()

    x_d = din("x", [NTOK, D])
    stC_d = din("stC", [NSEQ, NH, HD, HD])
    stn_d = din("stn", [NSEQ, NH, HD])
    stm_d = din("stm", [NSEQ, 128, NH])
    stR_d = din("stR", [NSEQ, NH, HD, HD])
    w_in_d = din("w_in", [D, DIN])
    w_out_d = din("w_out", [D, D])
    w_gate_d = din("w_gate", [D, DFF])
    w_up_d = din("w_up", [D, DFF])
    w_down_d = din("w_down", [DFF, D])
    cf_d = din("cf", [128, 416])
    gall_d = din("gall", [128, 73])
    cos_d = din("cosT", [128, NTOK])
    sin_d = din("sinT", [128, NTOK])

    y_d = dout("y", [NYTOK, D])
    oC_d = dout("oC", [NSEQ, NH, HD, HD])
    on_d = dout("on", [NSEQ, NH, HD])
    om_d = dout("om", [NSEQ, NH])
    oR_d = dout("oR", [NSEQ, NH, HD, HD])

    def sb(name, shape, dt=F32):
        return st.enter_context(nc.sbuf_tensor("s_" + name, list(shape), dt))

    def ps(name, shape, dt=F32):
        return st.enter_context(nc.psum_tensor("p_" + name, list(shape), dt))

    p = Prog()

    xT = sb("xT", [128, KC, 512], F32)
    actA = sb("actA", [128, KC, 512], BF16)
    actB = sb("actB", [128, KC, 512], BF16)
    wbuf = [sb(f"wbuf{i}", [128, 8192], BF16) for i in range(2)]
    gW = sb("gW", [128, KC, 8], BF16)
    stage = [sb(f"stage{i}", [128, 1024], F32) for i in range(2)]
    U = sb("U", [128, 22528], BF16)
    FM = U[:, 0:8192].rearrange("p (c t) -> p c t", t=512)
    TMv = U[:, 8192:8192 + 4 * 1028].rearrange("p (c h e) -> p c h e", c=4, h=4)
    TMo = U[:, 12304:12304 + 4096].rearrange("p (c e) -> p c e", c=4)
    mixtm = U[:, 16400:16400 + 1024]
    actT = U[:, 0:22528].rearrange("p (f t) -> p f t", t=512)
    XS = U[:, 0:16384].bitcast(F32).rearrange("p (c e) -> p c e", c=4)
    cf = sb("cf", [128, 416], F32)
    gall = sb("gall", [128, 73], F32)
    identb = sb("identb", [128, 128], BF16)
    onesb = sb("onesb", [128, 128], BF16)
    ropeC = sb("ropeC", [128, 512], F32)
    ropeS = sb("ropeS", [128, 512], F32)
    S32 = [sb(f"S32_{g}", [128, NH, 2, 257], F32) for g in range(2)]
    Sbf = [sb(f"Sbf_{g}", [128, NH, 2, 257], BF16) for g in range(2)]
    mcur = sb("mcur", [128, 4], F32)
    sqb = sb("sqb", [128, 4, 512], BF16)
    rt = [sb(f"rt{i}", [128, 512], F32) for i in range(4)]
    lnt = rt[0]
    rstd = rt[1]
    gsilu = sb("gsilu", [128, 4, 512], BF16)
    STb = sb("STb", [128, 4, 128], BF16)
    kg = sb("kg", [128, 4, 256], BF16)
    gsb = sb("gsb", [128, 4, 8], F32)
    spb = sb("spb", [128, 4, 4], F32)
    e1b = sb("e1b", [128, 4, 4], F32)
    nbb = sb("nbb", [128, 4, 8], F32)
    zb = sb("zb", [128, 4, 4], F32)
    zmb = sb("zmb", [128, 4, 4], F32)
    zm4 = sb("zm4", [128, 4], F32)
    dg4 = sb("dg4", [128, 4], F32)
    Mb = sb("Mb", [128, 4, 4], F32)
    ab = sb("ab", [128, 4, 4], F32)
    ub = sb("ub", [128, 4, 4], F32)
    dfb = sb("dfb", [128, 4, 4], F32)
    tmp4 = sb("tmp4", [128, 8, 4], F32)
    bnst = sb("bnst", [128, 4, 6], F32)
    bnag = sb("bnag", [128, 4, 2], F32)
    otmp = sb("otmp", [128, 256], F32)
    junk = sb("junk", [128, 256], BF16)
    numsb = sb("numsb", [128, 4, 257], F32)
    mnb = sb("mnb", [128, 4, 4], F32)
    mldb = sb("mldb", [128, 4, 4], F32)
    dg16 = sb("dg16", [128, 16], F32)
    zm16 = sb("zm16", [128, 1], F32)
    tmpg = sb("tmpg", [128, 4, 3, 4], F32)

    identf = cf[:, 0:128]
    causal = cf[:, 128:256]
    onesf = cf[:, 256:384]
    rc = cf[:, 384:408]
    c_eps = cf[:, 408:409]
    c_one = cf[:, 409:410]
    c_nln16 = cf[:, 410:411]
    c_zero = cf[:, 411:412]
    g1T = gall[:, 0:16]
    gmT = gall[:, 16:32]
    g2T = gall[:, 32:48]
    gFT = gall[:, 48:64]
    bgb = gall[:, 64:72]
    pmask = gall[:, 72:73]

    PS = [ps(f"ps{i}", [128, 512], F32) for i in range(7)]
    PSB = ps("psb", [128, 1024], BF16)
    R_PS = [Res(f"ps{i}") for i in range(7)]
    R_PSB = Res("psb")

    R_xT = [Res(f"xT{k}") for k in range(KC)]
    R_actA = [Res(f"actA{k}") for k in range(KC)]
    R_actB = [Res(f"actB{k}") for k in range(KC)]
    R_w = [Res("w0"), Res("w1")]
    R_gW = Res("gW")
    R_stage = [Res("stage0"), Res("stage1")]
    R_FM = [Res(f"FM{i}") for i in range(16)]
    R_TMv = [Res(f"TMv{i}") for i in range(4)]
    R_TMo = [Res(f"TMo{i}") for i in range(4)]
    R_mixtm = Res("mixtm")
    R_actT = [Res(f"actT{i}") for i in range(FC)]
    R_const = Res("const")
    R_rope = Res("rope")
    R_S32 = [[Res(f"S32_{g}_{h}") for h in range(NH)] for g in range(2)]
    R_Sbf = [[Res(f"Sbf_{g}_{h}") for h in range(NH)] for g in range(2)]
    R_m = Res("mcur")
    R_sqb = [Res(f"sqb{i}") for i in range(4)]
    R_rt = [Res(f"rt{i}") for i in range(4)]
    R_lnt = R_rt[0]
    R_rstd = R_rt[1]
    R_gsilu = [Res(f"gsilu{i}") for i in range(4)]
    R_STb = [Res(f"STb{h}") for h in range(4)]
    R_kg = [Res(f"kg{h}") for h in range(4)]
    R_gs = [Res(f"gs{c}") for c in range(4)]
    R_small = Res("small")
    R_otmp = Res("otmp")
    R_junk = Res("junk")
    R_xs = [Res(f"xs{c}") for c in range(4)]
    sem_xs = [Counter(f"D_xs{c}") for c in range(4)]
    R_num = [Res(f"num{h}") for h in range(4)]
    R_bn = [Res(f"bn{h}") for h in range(4)]
    R_den = [Res(f"den{h}") for h in range(4)]
    R_mld = Res("mld")
    sem_mld = Counter("D_mld")

    sem_const = Counter("D_const")
    sem_w = [Counter("D_w0"), Counter("D_w1")]
    sem_stage = [Counter("D_st0"), Counter("D_st1")]
    sem_rope = Counter("D_rope")
    sem_state = Counter("D_state")
    sem_out = Counter("D_out")
    out_toks = []

    U_A = R_FM + R_TMv + R_TMo + [R_mixtm]
    U_B = R_actT

    def mm(out, lhsT, rhs, start, stop, reads, writes, sig):
        p.op("pe", lambda e: e.matmul(out, lhsT, rhs, start=start, stop=stop), reads, writes, sig)

    def tr(out, in_, ident, reads, writes, sig=True):
        p.op("pe", lambda e: e.transpose(out, in_, ident), reads, writes, sig)

    def act(out, in_, func, reads, writes, scale=None, bias=None, accum=None):
        kw = {}
        if scale is not None:
            kw["scale"] = scale
        if bias is not None:
            kw["bias"] = bias
        if accum is not None:
            kw["accum_out"] = accum
        p.op("act", lambda e: e.activation(out=out, in_=in_, func=func, **kw), reads, writes)

    def tt(eng, out, in0, in1, op, reads, writes):
        p.op(eng, lambda e: e.tensor_tensor(out=out, in0=in0, in1=in1, op=op), reads, writes)

    def ts(eng, out, in0, s1, s2, op0, op1, reads, writes):
        if op1 is None:
            p.op(eng, lambda e: e.tensor_scalar(out=out, in0=in0, scalar1=s1, scalar2=None, op0=op0),
                 reads, writes)
        else:
            p.op(eng, lambda e: e.tensor_scalar(out=out, in0=in0, scalar1=s1, scalar2=s2, op0=op0, op1=op1),
                 reads, writes)

    def stt(out, in0, scalar, in1, op0, op1, reads, writes):
        p.op("dve", lambda e: e.scalar_tensor_tensor(out=out, in0=in0, scalar=scalar, in1=in1, op0=op0, op1=op1),
             reads, writes)

    def cp(eng, out, in_, reads, writes):
        if eng == "act":
            act(out, in_, AF.Copy, reads, writes)
        else:
            p.op(eng, lambda e: e.tensor_copy(out=out, in_=in_), reads, writes)

    def scp(eng, out, in_, sc_ap, reads, writes):
        if eng == "act":
            act(out, in_, AF.Copy, reads, writes, scale=sc_ap)
        else:
            ts("dve", out, in_, sc_ap, None, ALU.mult, None, reads, writes)

    evq = [0]

    def ev_eng():
        evq[0] += 1
        return "act" if evq[0] % 2 else "dve"

    psq = [0]

    def next_ps(n=4):
        psq[0] = (psq[0] + 1) % n
        return psq[0]

    p.dma_group("sp", [(cf[:, :], cf_d[:, :], {}), (gall[:, :], gall_d[:, :], {})], [], [R_const], sem_const)
    cp("dve", identb[:, :], identf, [R_const], [R_const])
    cp("dve", onesb[:, :], onesf, [R_const], [R_const])
    p.dma("pool", gW[:, :, :], w_in_d[:, 8192:8200].rearrange("(k p) c -> p k c", p=128),
          [], [R_gW], Counter("D_gw"))

    wslot = [0]

    wsc = nc.dram_tensor("wsc", [58, 128, 8192], BF16).ap()
    slab_ids = {}
    R_sc = {}
    sem_wb = [Counter("D_wb0"), Counter("D_wb1")]

    def load_slab(key, src_ap, n, shape_fn):
        s = wslot[0]
        wslot[0] ^= 1
        if key not in slab_ids:
            idx = len(slab_ids)
            slab_ids[key] = idx
            R_sc[idx] = Res(f"sc{idx}")
            p.dma("pool", shape_fn(wbuf[s][:, 0:n]), src_ap, [], [R_w[s]], sem_w[s])
            p.dma("sp", wsc[idx, :, 0:n], wbuf[s][:, 0:n], [R_w[s]], [R_sc[idx]], sem_wb[s])
        else:
            idx = slab_ids[key]
            p.dma("pool", wbuf[s][:, 0:n], wsc[idx, :, 0:n], [R_sc[idx]], [R_w[s]], sem_w[s])
        return s

    def slab_k16(wd, c0, ncol=512, name=None):
        src = wd[:, c0:c0 + ncol].rearrange("(k p) c -> p k c", p=128)
        fn = lambda a: a.rearrange("p (k c) -> p k c", c=ncol)
        s = load_slab((wd.tensor.name, c0), src, 16 * ncol, fn)
        return s, fn(wbuf[s][:, 0:16 * ncol])

    def slab_down(c0):
        src = w_down_d[:, c0:c0 + 128].rearrange("(k p) c -> p k c", p=128)
        fn = lambda a: a.rearrange("p (k c) -> p k c", c=128)
        s = load_slab(("w_down", c0), src, FC * 128, fn)
        return s, fn(wbuf[s][:, 0:FC * 128])

    sem_state_g = [Counter("D_state0"), Counter("D_state1")]
    sem_out_g = [Counter("D_out0"), Counter("D_out1")]

    def load_state(seq, g):
        items = []
        src = stC_d if g == 0 else stR_d
        for h in range(NH):
            items.append((S32[g][:, h, :, 0:256], src[seq, h].rearrange("(a p) e -> p a e", p=128), {}))
            if g == 0:
                items.append((S32[0][:, h, :, 256:257], stn_d[seq, h].rearrange("(a p o) -> p a o", p=128, o=1),
                              dict(allow_slow_non_contiguous=True)))
        p.dma_group("sp", items, [], R_S32[g], sem_state_g[g])

    def store_state(seq, g, ci=0):
        items = []
        dst = oC_d if g == 0 else oR_d
        for h in range(NH):
            items.append((dst[seq, h].rearrange("(a p) e -> p a e", p=128), S32[g][:, h, :, 0:256], {}))
            if g == 0:
                items.append((on_d[seq, h].rearrange("(a p o) -> p a o", p=128, o=1), S32[0][:, h, :, 256:257],
                              dict(allow_slow_non_contiguous=True)))
        if g == 0:
            items.append((om_d[seq:seq + 1, :], mnb[0:1, ci, :], {}))
        out_toks.append(p.dma_group("sp", items, R_S32[g] + ([R_gs[ci]] if g == 0 else []), [], sem_out_g[g]))

    def mask_state(g):
        for h in range(NH):
            nv = 257 if g == 0 else 256
            ts("dve", S32[g][:, h, :, 0:nv], S32[g][:, h, :, 0:nv], pmask, None, ALU.mult, None,
               [R_S32[g][h], R_const], [R_S32[g][h]])

    def prefetch_x(nblk):
        for ci, ch in enumerate(nblk):
            for r in U_A + U_B:
                if r.w is not None:
                    R_xs[ci].r.append(r.w)
                R_xs[ci].r.extend(r.r)
            p.dma("sp", XS[:ch["L"], ci, :], x_d[ch["tok0"]:ch["tok0"] + ch["L"], :], [], [R_xs[ci]], sem_xs[ci])

    import os
    KSTOP = float(os.environ.get("KSTOP", "0"))

    class _Stop(Exception):
        pass

    def ckpt(n):
        if KSTOP and n >= KSTOP:
            raise _Stop()

    prefetch_x(blocks[0])
    try:
      for bi, blk in enumerate(blocks):
          NB = sum(ch["L"] for ch in blk)
          t0 = blk[0]["tok0"]
          nch = len(blk)

          p.dma_group("sp", [(ropeC[:, 0:NB], cos_d[:, t0:t0 + NB], {}), (ropeS[:, 0:NB], sin_d[:, t0:t0 + NB], {})],
                      [], [R_rope], sem_rope)

          for ci, ch in enumerate(blk):
              L, bo = ch["L"], ch["boff"]
              for hf in range(2):
                  for q in range(2):
                      b = next_ps()
                      for j in range(4):
                          kk = hf * 8 + q * 4 + j
                          tr(PS[b][:, j * 128:j * 128 + L], XS[:L, ci, kk * 128:(kk + 1) * 128],
                             identf[:L, :L], [R_xs[ci], R_const], [R_PS[b]], sig=(j == 3))
                      k0 = hf * 8 + q * 4
                      cp(ev_eng(), xT[:, k0:k0 + 4, bo:bo + L],
                         PS[b][:, :].rearrange("p (k t) -> p k t", t=128)[:, :, 0:L],
                         [R_PS[b]], R_xT[k0:k0 + 4])

          for r in U_A + U_B:
              for rx in R_xs:
                  r.r.extend(rx.r)
          sqq = []

          def sq_acc(k):
              s_ = k % 4
              act(sqb[:, s_, 0:NB], xT[:, k, 0:NB], AF.Square, [R_xT[k]], [R_sqb[s_]])
              mm(PS[4][:, 0:NB], onesb[:, :], sqb[:, s_, 0:NB], k == 0, k == KC - 1,
                 [R_sqb[s_], R_const], [R_PS[4]], sig=True)

          def sq_push(k, lag=6):
              sqq.append(k)
              while len(sqq) > lag:
                  sq_acc(sqq.pop(0))

          def sq_flush():
              while sqq:
                  sq_acc(sqq.pop(0))

          def fm_norm(gT, dst, R_dst, pre=False):
              b = 4 if pre else next_ps()
              for k in range(0 if pre else KC):
                  s = k % 4
                  act(sqb[:, s, 0:NB], xT[:, k, 0:NB], AF.Square, [R_xT[k]], [R_sqb[s]])
                  mm(PS[b][:, 0:NB], onesb[:, :], sqb[:, s, 0:NB], k == 0, k == KC - 1,
                     [R_sqb[s], R_const], [R_PS[b]], sig=True)
              act(lnt[:, 0:NB], PS[b][:, 0:NB], AF.Ln, [R_PS[b], R_const], [R_lnt], scale=1.0 / D, bias=c_eps)
              act(rstd[:, 0:NB], lnt[:, 0:NB], AF.Exp, [R_lnt], [R_rstd], scale=-0.5)
              for k in range(KC):
                  stt(dst[:, k, 0:NB], xT[:, k, 0:NB], gT[:, k:k + 1], rstd[:, 0:NB], ALU.mult, ALU.mult,
                      [R_xT[k], R_rstd, R_const], [R_dst[k]])

          ckpt(1)
          fm_norm(g1T, actA, R_actA)
          ckpt(2)

          for ch in blk:
              pass

          def inproj_fm_slab(c0, cb0, rope):
              s, W = slab_k16(w_in_d, c0)
              if not rope:
                  for j in range(4):
                      b = next_ps()
                      for k in range(KC):
                          mm(PS[b][:, 0:NB], W[:, k, j * 128:(j + 1) * 128], actA[:, k, 0:NB], k == 0, k == KC - 1,
                             [R_w[s], R_actA[k]], [R_PS[b]], sig=(k == KC - 1))
                      cp(ev_eng(), FM[:, cb0 + j, 0:NB], PS[b][:, 0:NB], [R_PS[b]], [R_FM[cb0 + j]])
              else:
                  for pr in range(2):
                      bs = []
                      for hf in range(2):
                          j = pr * 2 + hf
                          b = next_ps()
                          bs.append(b)
                          for k in range(KC):
                              mm(PS[b][:, 0:NB], W[:, k, j * 128:(j + 1) * 128], actA[:, k, 0:NB], k == 0, k == KC - 1,
                                 [R_w[s], R_actA[k]], [R_PS[b]], sig=(k == KC - 1))
                      b1, b2 = bs
                      P1, P2 = PS[b1][:, 0:NB], PS[b2][:, 0:NB]
                      cs_, sn_ = ropeC[:, 0:NB], ropeS[:, 0:NB]
                      tt("dve", rt[0][:, 0:NB], P1, cs_, ALU.mult, [R_PS[b1], R_rope], [R_rt[0]])
                      tt("dve", rt[1][:, 0:NB], P2, sn_, ALU.mult, [R_PS[b2], R_rope], [R_rt[1]])
                      tt("dve", rt[2][:, 0:NB], P2, cs_, ALU.mult, [R_PS[b2], R_rope], [R_rt[2]])
                      tt("dve", rt[3][:, 0:NB], P1, sn_, ALU.mult, [R_PS[b1], R_rope], [R_rt[3]])
                      tt("dve", FM[:, cb0 + pr * 2, 0:NB], rt[0][:, 0:NB], rt[1][:, 0:NB], ALU.subtract,
                         [R_rt[0], R_rt[1]], [R_FM[cb0 + pr * 2]])
                      tt("dve", FM[:, cb0 + pr * 2 + 1, 0:NB], rt[2][:, 0:NB], rt[3][:, 0:NB], ALU.add,
                         [R_rt[2], R_rt[3]], [R_FM[cb0 + pr * 2 + 1]])

          def inproj_tm_slab(c0, kind, hh):
              s, W = slab_k16(w_in_d, c0)
              for ci, ch in enumerate(blk):
                  L, bo = ch["L"], ch["boff"]
                  b = next_ps()
                  for k in range(KC):
                      mm(PS[b][:L, 0:512], actA[:, k, bo:bo + L], W[:, k, 0:512], k == 0, k == KC - 1,
                         [R_w[s], R_actA[k]], [R_PS[b]], sig=(k == KC - 1))
                  if kind == "v":
                      cp(ev_eng(), TMv[:L, ci, hh * 2:hh * 2 + 2, 0:256],
                         PS[b][:L, :].rearrange("p (h e) -> p h e", e=256), [R_PS[b]], [R_TMv[ci]])
                  elif kind == "sig":
                      act(TMo[:L, ci, hh * 512:(hh + 1) * 512], PS[b][:L, 0:512], AF.Sigmoid, [R_PS[b]], [R_TMo[ci]])
                  else:
                      act(TMo[:L, ci, hh * 512:(hh + 1) * 512], PS[b][:L, 0:512], AF.Silu, [R_PS[b]], [R_TMo[ci]])

          def mixer_chunk(g, ci, ch, state_only=False, phase=None):
              PA1 = phase in (None, 'A1')
              PA2 = phase in (None, 'A2')
              PB = phase in (None, 'B')
              L, bo = ch["L"], ch["boff"]
              NV = 257 if g == 0 else 256
              RS32, RSbf = R_S32[g], R_Sbf[g]
              if g == 0:
                  us = [ub[:L, ci, h:h + 1] for h in range(NH)]
                  ug = us
                  a_upd = [ab[:, ci, h:h + 1] for h in range(NH)]
              else:
                  us = [rc[:L, h:h + 1] for h in range(NH)]
                  ugc = 8 if L == 128 else 12
                  cdc = 16 if L == 128 else 20
                  ug = [rc[:L, ugc + h:ugc + h + 1] for h in range(NH)]
                  a_upd = [rc[:, cdc + h:cdc + h + 1] for h in range(NH)]
              Rgs = [R_gs[ci], R_const]
              bS = 4
              for h in range(NH if (not state_only and PA1) else 0):
                  for hf in range(2):
                      cbk = 8 + h * 2 + hf
                      cbq = h * 2 + hf
                      mm(PS[bS][:L, h * 128:h * 128 + L], FM[:, cbk, bo:bo + L], FM[:, cbq, bo:bo + L],
                         hf == 0, hf == 1, [R_FM[cbk], R_FM[cbq]], [R_PS[bS]], sig=(hf == 1))
              for h in range(NH if PA1 else 0):
                  for hf in range(2):
                      cbk = 8 + h * 2 + hf
                      tr(PSB[:L, h * 256 + hf * 128:h * 256 + (hf + 1) * 128], FM[:, cbk, bo:bo + L],
                         identb[:, :], [R_FM[cbk], R_const], [R_PSB], sig=(hf == 1))
              ckpt(3.6)
              for h in range(NH if (not state_only and PA1) else 0):
                  stt(STb[:L, h, 0:L], PS[bS][:L, h * 128:h * 128 + L], us[h], causal[:L, :L], ALU.mult, ALU.mult,
                      [R_PS[bS]] + Rgs, [R_STb[h]])
              for h in range(NH if PA1 else 0):
                  act(kg[:L, h, :], PSB[:L, h * 256:(h + 1) * 256], AF.Copy, [R_PSB] + Rgs, [R_kg[h]], scale=ug[h])
              ckpt(3.7)
              for h in range(NH if (not state_only and PA1) else 0):
                  if g == 0:
                      act(Sbf[g][:, h, :, :], S32[g][:, h, :, :], AF.Copy, [RS32[h]] + Rgs, [RSbf[h]], scale=a_upd[h])
                  else:
                      cp("pool", Sbf[g][:, h, :, 0:256], S32[g][:, h, :, 0:256], [RS32[h]], [RSbf[h]])
              for h in range(NH if (not state_only and PA2) else 0):
                  bn = h
                  mm(PS[bn][:L, 0:NV], STb[:L, h, 0:L], TMv[:L, ci, h, 0:NV], True, False,
                     [R_STb[h], R_TMv[ci]], [R_PS[bn]], sig=False)
                  for hf in range(2):
                      cbq = h * 2 + hf
                      mm(PS[bn][:L, 0:NV], FM[:, cbq, bo:bo + L], Sbf[g][:, h, hf, 0:NV], False, hf == 1,
                         [R_FM[cbq], RSbf[h]], [R_PS[bn]], sig=(hf == 1))
                  cp("act", numsb[:L, h, 0:NV], PS[bn][:L, 0:NV], [R_PS[bn]], [R_num[h]])
              ckpt(3.8)
              for h in range(NH if PA1 else 0):
                  for hf in range(2):
                      bu = [5, 6, 0, 1, 2, 3, 5, 6][h * 2 + hf]
                      mm(PS[bu][:, 0:NV], kg[:L, h, hf * 128:(hf + 1) * 128], TMv[:L, ci, h, 0:NV], True, True,
                         [R_kg[h], R_TMv[ci]], [R_PS[bu]], sig=True)
                      stt(S32[g][:, h, hf, 0:NV], S32[g][:, h, hf, 0:NV], a_upd[h], PS[bu][:, 0:NV],
                          ALU.mult, ALU.add, [R_PS[bu], RS32[h]] + Rgs, [RS32[h]])
              ckpt(3.9)
              if state_only or not PB:
                  return
              if g == 0:
                  den = tmp4[:L, 0, :]
                  rr = tmp4[:L, 1, :]
                  ssq = tmp4[:L, 2, :]
                  t1 = tmp4[:L, 3, :]
                  lnv = tmp4[:L, 4, :]
                  rs = tmp4[:L, 5, :]
                  sc = tmp4[:L, 6, :]
                  for h in range(NH):
                      act(tmp4[:L, 0, h:h + 1], numsb[:L, h, 256:257], AF.Abs, [R_num[h]], [R_den[h]])
                      p.op("dve", lambda e, h=h: e.bn_stats(out=bnst[:L, h, :], in_=numsb[:L, h, 0:256]),
                           [R_num[h]], [R_bn[h]])
                  for h in range(NH):
                      p.op("dve", lambda e, h=h: e.bn_aggr(out=bnag[:L, h, :], in_=bnst[:L, h, :]),
                           [R_bn[h]], [R_bn[h]])
                  tt("dve", den, den, dfb[:L, ci, :], ALU.max, R_den + R_bn + [R_small] + Rgs, R_den + [R_small])
                  p.op("dve", lambda e: e.reciprocal(out=rr, in_=den), [R_small], [R_small])
                  tt("dve", ssq, bnag[:L, :, 0], bnag[:L, :, 0], ALU.mult, R_bn + [R_small], [R_small])
                  tt("dve", ssq, ssq, bnag[:L, :, 1], ALU.add, R_bn + [R_small], [R_small])
                  tt("dve", t1, ssq, rr, ALU.mult, [R_small], [R_small])
                  tt("dve", t1, t1, rr, ALU.mult, [R_small], [R_small])
                  act(lnv, t1, AF.Ln, [R_small, R_const], [R_small], bias=c_eps[:L, :])
                  act(rs, lnv, AF.Exp, [R_small], [R_small], scale=-0.5)
                  tt("dve", sc, rs, rr, ALU.mult, [R_small], [R_small])
                  for h in range(NH):
                      stt(mixtm[:L, h * 256:(h + 1) * 256], numsb[:L, h, 0:256], tmp4[:L, 6, h:h + 1],
                          TMo[:L, ci, h * 256:(h + 1) * 256], ALU.mult, ALU.mult,
                          [R_num[h], R_small, R_TMo[ci]], [R_mixtm])
              else:
                  for h in range(NH):
                      p.op("dve", lambda e, h=h: e.bn_stats(out=bnst[:L, h, :], in_=numsb[:L, h, 0:256]),
                           [R_num[h]], [R_bn[h]])
                  for h in range(NH):
                      p.op("dve", lambda e, h=h: e.bn_aggr(out=bnag[:L, h, :], in_=bnst[:L, h, :]),
                           [R_bn[h]], [R_bn[h]])
                  inter = rc[:L, 4:8]
                  var = bnag[:L, :, 1]
                  t1 = tmp4[:L, 3, :]
                  lnv = tmp4[:L, 4, :]
                  rs = tmp4[:L, 5, :]
                  sc = tmp4[:L, 6, :]
                  tt("dve", t1, var, inter, ALU.mult, R_bn + [R_small, R_const], R_bn + [R_small])
                  tt("dve", t1, t1, inter, ALU.mult, [R_small, R_const], [R_small])
                  act(lnv, t1, AF.Ln, [R_small, R_const], [R_small], bias=c_eps[:L, :])
                  act(rs, lnv, AF.Exp, [R_small], [R_small], scale=-0.5)
                  tt("dve", sc, rs, inter, ALU.mult, [R_small, R_const], [R_small])
                  for h in range(NH):
                      stt(otmp[:L, :], numsb[:L, h, 0:256], bnag[:L, h, 0:1], TMo[:L, ci, h * 256:(h + 1) * 256],
                          ALU.subtract, ALU.mult, [R_num[h], R_bn[h], R_small, R_TMo[ci]], [R_otmp])
                      act(mixtm[:L, h * 256:(h + 1) * 256], otmp[:L, :], AF.Copy, [R_otmp, R_small], [R_mixtm],
                          scale=tmp4[:L, 6, h:h + 1])
              ckpt(3.95)
              for f in range(8):
                  tr(PSB[:, f * 128:f * 128 + L], mixtm[:L, f * 128:(f + 1) * 128], identb[:L, :L],
                     [R_mixtm, R_const], [R_PSB], sig=(f == 7))
              ckpt(3.97)
              for f in range(8):
                  kf = g * 8 + f
                  scp("act", actB[:, kf, bo:bo + L], PSB[:, f * 128:f * 128 + L], gmT[:, kf:kf + 1],
                      [R_PSB, R_const], [R_actB[kf]])
              ckpt(3.98)

          PRE = blk[0]["prefix"]
          b = 4
          fl = lambda t: t.rearrange("p c h -> p (c h)")
          groups = []
          for ci, ch in enumerate(blk):
              if groups and blk[groups[-1][0]]["L"] == ch["L"]:
                  groups[-1][1] = ci + 1
              else:
                  groups.append([ci, ci + 1])
          gstages = []
          for c0, c1 in groups:
              def mk_stages(c0=c0, c1=c1):
                  L = blk[c0]["L"]
                  ng = c1 - c0
                  n4 = ng * 4
                  Rg = R_gs[c0:c1]

                  def g1():
                      for ci in range(c0, c1):
                          bo = blk[ci]["boff"]
                          for k in range(KC):
                              mm(PS[b][:L, ci * 8:(ci + 1) * 8], actA[:, k, bo:bo + L], gW[:, k, :], k == 0, k == KC - 1,
                                 [R_gW, R_actA[k]], [R_PS[b]], sig=(k == KC - 1))
                      for ci in range(c0, c1):
                          tt("dve", gsb[:L, ci, :], PS[b][:L, ci * 8:(ci + 1) * 8], bgb[:L, :], ALU.add,
                             [R_PS[b], R_const], [R_gs[ci]])
                      act(e1b[:L, c0:c1, :], gsb[:L, c0:c1, 4:8], AF.Exp, Rg, Rg, scale=-1.0)
                      act(spb[:L, c0:c1, :], e1b[:L, c0:c1, :], AF.Ln, Rg + [R_const], Rg, bias=c_one[:L, :])

                  def g2():
                      mm(PS[b][:L, 64:64 + n4], causal[:L, :L], fl(spb[:L, c0:c1, :]), True, True, Rg + [R_const], [R_PS[b]], True)
                      mm(PS[b][:, 128:128 + n4], onesf[:L, :], fl(spb[:L, c0:c1, :]), True, True, Rg + [R_const], [R_PS[b]], True)
                      tt("dve", zb[:L, c0:c1, :], gsb[:L, c0:c1, 0:4],
                         PS[b][:L, 64:64 + n4].rearrange("p (c h) -> p c h", h=4), ALU.add, [R_PS[b]] + Rg, Rg)
                      cp("act", nbb[:L, c0:c1, 0:4], PS[b][:L, 64:64 + n4].rearrange("p (c h) -> p c h", h=4), [R_PS[b]], Rg)
                      cp("act", nbb[:, c0:c1, 4:8], PS[b][:, 128:128 + n4].rearrange("p (c h) -> p c h", h=4), [R_PS[b]], Rg)

                  def g3():
                      tr(PS[b][0:n4, 256:256 + L], fl(zb[:L, c0:c1, :]), identf[:L, :L], Rg + [R_const], [R_PS[b]])
                      p.op("dve", lambda e, b=b, L=L, n4=n4: e.tensor_reduce(out=zm16[0:n4, 0:1],
                                                                              in_=PS[b][0:n4, 256:256 + L],
                                                                              axis=AX.X, op=ALU.max),
                           [R_PS[b]], [R_small])
                      ts("dve", dg16[0:n4, 0:n4], identf[0:n4, 0:n4], zm16[0:n4, 0:1], None, ALU.mult, None,
                         [R_small, R_const], [R_small])

                  def g4():
                      mm(PS[b][:, 384:384 + n4], onesf[0:n4, :], dg16[0:n4, 0:n4], True, True, [R_small, R_const], [R_PS[b]], True)
                      cp("act", zmb[:, c0:c1, :], PS[b][:, 384:384 + n4].rearrange("p (c h) -> p c h", h=4), [R_PS[b]], Rg)
                  return [g1, g2, g3, g4]
              gstages += mk_stages()

          def g5():
              firsts = [(ci, ch) for ci, ch in enumerate(blk) if ch["first"]]
              if firsts:
                  p.dma_group("sp", [(mldb[:, ci, :], stm_d[ch["seq"]], {}) for ci, ch in firsts], [], [R_mld], sem_mld)
              for ci, ch in enumerate(blk):
                  if ch["first"]:
                      mprev, Rm = mldb[:, ci, :], R_mld
                  elif ci == 0:
                      mprev, Rm = mcur[:, :], R_m
                  else:
                      mprev, Rm = mnb[:, ci - 1, :], R_gs[ci - 1]
                  G = R_gs[ci]
                  L = ch["L"]
                  tt("dve", Mb[:, ci, :], mprev, zmb[:, ci, :], ALU.max, [Rm, G], [G])
                  tt("dve", tmpg[:, ci, 0, :], mprev, Mb[:, ci, :], ALU.subtract, [Rm, G], [G])
                  act(ab[:, ci, :], tmpg[:, ci, 0, :], AF.Exp, [G], [G])
                  tt("dve", tmpg[:L, ci, 1, :], zb[:L, ci, :], Mb[:L, ci, :], ALU.subtract, [G], [G])
                  act(ub[:L, ci, :], tmpg[:L, ci, 1, :], AF.Exp, [G, R_const], [G], bias=c_nln16[:L, :])
                  tt("dve", tmpg[:L, ci, 2, :], nbb[:L, ci, 0:4], Mb[:L, ci, :], ALU.subtract, [G], [G])
                  act(dfb[:L, ci, :], tmpg[:L, ci, 2, :], AF.Exp, [G], [G])
                  tt("dve", mnb[:, ci, :], Mb[:, ci, :], nbb[:, ci, 4:8], ALU.subtract, [G], [G])
              if blk[-1]["endpre"]:
                  ts("dve", mcur[:, :], mnb[:, nch - 1, :], pmask, None, ALU.mult, None, [R_gs[nch - 1], R_const], [R_m])
              else:
                  cp("dve", mcur[:, :], mnb[:, nch - 1, :], [R_gs[nch - 1]], [R_m])
          gstages.append(g5)

          def gstep():
              if gstages:
                  gstages.pop(0)()

          gstep()
          if not PRE:
              inproj_fm_slab(0, 0, False)
              gstep()
              inproj_fm_slab(512, 4, False)
              gstep()
          inproj_fm_slab(1024, 8, False)
          gstep()
          inproj_fm_slab(1536, 12, False)
          gstep()
          for c in range(nch):
              p.op("dve", lambda e, c=c: e.memset(TMv[:, c, :, 256:257], 1.0), [], [R_TMv[c]])
          inproj_tm_slab(2048, "v", 0)
          gstep()
          inproj_tm_slab(2560, "v", 1)
          gstep()
          if not PRE:
              inproj_tm_slab(3072, "sig", 0)
              gstep()
              inproj_tm_slab(3584, "sig", 1)
          while gstages:
              gstep()
          ckpt(3)
          ckpt(3.5)
          for ci, ch in enumerate(blk):
              if ch["first"]:
                  load_state(ch["seq"], 0)
              mixer_chunk(0, ci, ch, state_only=PRE, phase="A1")
              if ch["endpre"]:
                  assert ci == nch - 1
                  mask_state(0)
              if ch["last"]:
                  store_state(ch["seq"], 0, ci)
              if not PRE:
                  if ci > 0:
                      mixer_chunk(0, ci - 1, blk[ci - 1], phase="B")
                  mixer_chunk(0, ci, ch, phase="A2")
          if not PRE:
              inproj_fm_slab(4096, 0, True)
              mixer_chunk(0, nch - 1, blk[-1], phase="B")

          ckpt(4)
          if not PRE:
              inproj_fm_slab(4608, 4, True)
          inproj_fm_slab(5120, 8, True)
          inproj_fm_slab(5632, 12, True)
          inproj_tm_slab(6144, "v", 0)
          inproj_tm_slab(6656, "v", 1)
          if not PRE:
              inproj_tm_slab(7168, "silu", 0)
              inproj_tm_slab(7680, "silu", 1)
          for ci, ch in enumerate(blk):
              if ch["first"]:
                  load_state(ch["seq"], 1)
              mixer_chunk(1, ci, ch, state_only=PRE, phase="A1")
              if ch["endpre"]:
                  mask_state(1)
              if ch["last"]:
                  store_state(ch["seq"], 1)
              if not PRE:
                  if ci > 0:
                      mixer_chunk(1, ci - 1, blk[ci - 1], phase="B")
                  mixer_chunk(1, ci, ch, phase="A2")
          if not PRE:
              s0, W0 = slab_k16(w_out_d, 0)
              for j in range(4):
                  for k in range(8):
                      mm(PS[j][:, 0:NB], W0[:, k, j * 128:(j + 1) * 128], actB[:, k, 0:NB], k == 0, False,
                         [R_w[s0], R_actB[k]], [R_PS[j]], sig=False)
              mixer_chunk(1, nch - 1, blk[-1], phase="B")
              for j in range(4):
                  for k in range(8, KC):
                      mm(PS[j][:, 0:NB], W0[:, k, j * 128:(j + 1) * 128], actB[:, k, 0:NB], False, k == KC - 1,
                         [R_w[s0], R_actB[k]], [R_PS[j]], sig=(k == KC - 1))
                  tt("dve", xT[:, j, 0:NB], xT[:, j, 0:NB], PS[j][:, 0:NB], ALU.add, [R_PS[j], R_xT[j]], [R_xT[j]])
                  sq_push(j)
          if PRE:
              if bi + 1 < len(blocks):
                  prefetch_x(blocks[bi + 1])
              continue

          ckpt(5)
          for sl in range(1, 4):
              s, W = slab_k16(w_out_d, sl * 512)
              for j in range(4):
                  cb = sl * 4 + j
                  b = next_ps()
                  for k in range(KC):
                      mm(PS[b][:, 0:NB], W[:, k, j * 128:(j + 1) * 128], actB[:, k, 0:NB], k == 0, k == KC - 1,
                         [R_w[s], R_actB[k]], [R_PS[b]], sig=(k == KC - 1))
                  tt("dve", xT[:, cb, 0:NB], xT[:, cb, 0:NB], PS[b][:, 0:NB], ALU.add, [R_PS[b], R_xT[cb]], [R_xT[cb]])
                  sq_push(cb)

          ckpt(6)
          sq_flush()
          fm_norm(g2T, actA, R_actA, pre=True)

          for i in range(11):
              s, W = slab_k16(w_gate_d, i * 512)
              for j in range(4):
                  b = next_ps()
                  for k in range(KC):
                      mm(PS[b][:, 0:NB], W[:, k, j * 128:(j + 1) * 128], actA[:, k, 0:NB], k == 0, k == KC - 1,
                         [R_w[s], R_actA[k]], [R_PS[b]], sig=(k == KC - 1))
                  act(gsilu[:, j, 0:NB], PS[b][:, 0:NB], AF.Silu, [R_PS[b]], [R_gsilu[j]])
              s, W = slab_k16(w_up_d, i * 512)
              for j in range(4):
                  f = i * 4 + j
                  b = next_ps()
                  for k in range(KC):
                      mm(PS[b][:, 0:NB], W[:, k, j * 128:(j + 1) * 128], actA[:, k, 0:NB], k == 0, k == KC - 1,
                         [R_w[s], R_actA[k]], [R_PS[b]], sig=(k == KC - 1))
                  tt("dve", actT[:, f, 0:NB], PS[b][:, 0:NB], gsilu[:, j, 0:NB], ALU.mult,
                     [R_PS[b], R_gsilu[j]], [R_actT[f]] + (U_A if f == 0 else []))

          ckpt(7)
          for cb in range(KC):
              s, W = slab_down(cb * 128)
              b = next_ps()
              for f in range(FC):
                  mm(PS[b][:, 0:NB], W[:, f, :], actT[:, f, 0:NB], f == 0, f == FC - 1,
                     [R_w[s], R_actT[f]], [R_PS[b]], sig=(f == FC - 1))
              tt("dve", xT[:, cb, 0:NB], xT[:, cb, 0:NB], PS[b][:, 0:NB], ALU.add, [R_PS[b], R_xT[cb]], [R_xT[cb]])
              sq_push(cb)
          sq_flush()

          if bi + 1 < len(blocks):
              prefetch_x(blocks[bi + 1])
          ckpt(8)
          b = 4
          act(lnt[:, 0:NB], PS[b][:, 0:NB], AF.Ln, [R_PS[b], R_const], [R_lnt], scale=1.0 / D, bias=c_eps)
          act(rstd[:, 0:NB], lnt[:, 0:NB], AF.Exp, [R_lnt], [R_rstd], scale=-0.5)
          for k in range(KC):
              stt(xT[:, k, 0:NB], xT[:, k, 0:NB], gFT[:, k:k + 1], rstd[:, 0:NB], ALU.mult, ALU.mult,
                  [R_xT[k], R_rstd, R_const], [R_xT[k]])
          for ci, ch in enumerate(blk):
              L, bo = ch["L"], ch["boff"]
              for hf in range(2):
                  sg = (2 * ci + hf) % 2
                  for q in range(2):
                      b = next_ps()
                      for j in range(4):
                          kk = hf * 8 + q * 4 + j
                          tr(PS[b][:L, j * 128:(j + 1) * 128], xT[:, kk, bo:bo + L], identf[:, :],
                             [R_xT[kk], R_const], [R_PS[b]], sig=(j == 3))
                      cp(ev_eng(), stage[sg][:L, q * 512:(q + 1) * 512], PS[b][:L, 0:512], [R_PS[b]], [R_stage[sg]])
                  out_toks.append(p.dma("sp", y_d[ch["ytok0"]:ch["ytok0"] + L, hf * 1024:(hf + 1) * 1024],
                                        stage[sg][:L, :], [R_stage[sg]], [], sem_stage[sg]))

          for r in U_A:
              for rb in U_B:
                  r.r.extend(rb.r)

    except _Stop:
        pass
    last = {}
    for k, v in out_toks:
        last[k] = max(last.get(k, 0), v)
    p.wait_all("sp", list(last.items()))
    p.emit(nc, st)
    st.close()
    return nc, NTOK, NYTOK, NSEQ


_CACHE = {}


def _consts(pos):
    cf = np.zeros((128, 416), np.float32)
    cf[:, 0:128] = np.eye(128, dtype=np.float32)
    s = np.arange(128)
    cf[:, 128:256] = (s[:, None] <= s[None, :]).astype(np.float32)
    cf[:, 256:384] = 1.0
    lg = np.log(1.0 - np.exp2(-5.0 - np.arange(4, dtype=np.float64)))
    sc = 1.0 / 16.0
    cf[:, 384:388] = np.exp(-lg[None, :] * (s[:, None] + 1.0)) * sc
    cf[:, 388:392] = np.exp(lg[None, :] * (s[:, None] + 1.0))
    cf[:, 392:396] = np.exp(lg[None, :] * (127.0 - s[:, None])) * sc
    cf[:, 396:400] = np.exp(lg[None, :] * (63.0 - s[:, None])) * sc
    cf[:, 400:404] = np.exp(lg[None, :] * 128.0)
    cf[:, 404:408] = np.exp(lg[None, :] * 64.0)
    cf[:, 408] = EPS
    cf[:, 409] = 1.0
    cf[:, 410] = -math.log(16.0)
    pos = np.asarray(pos, np.float64)
    freqs = 10000.0 ** (-np.arange(128, dtype=np.float64) / 128)
    ang = pos[None, :] * freqs[:, None]
    return cf, np.cos(ang).astype(np.float32), np.sin(ang).astype(np.float32)


def core_positions(TPRE, TP, NS, half):
    return np.concatenate([np.arange(TPRE, dtype=np.float32), half * TP + np.arange(TP, dtype=np.float32)] +
                          [PAST_LEN + np.arange(64, dtype=np.float32) for _ in range(NS)])


def make_gall(g_norm1, g_mlstm_norm, g_ret_norm, g_norm2, g_final, b_gates, maskval):
    f = lambda a: np.asarray(a, dtype=np.float32)
    gm = np.concatenate([f(g_mlstm_norm).reshape(-1), f(g_ret_norm).reshape(-1)])
    fm = lambda v: np.ascontiguousarray(f(v).reshape(16, 128).T)
    gall = np.concatenate([fm(g_norm1), fm(gm), fm(g_norm2), fm(g_final),
                           np.broadcast_to(f(b_gates).reshape(1, 8), (128, 8)),
                           np.full((128, 1), maskval, np.float32)], axis=1)
    return np.ascontiguousarray(gall, dtype=np.float32)


_CACHE = {}


def kernel(x_prompt, x_sample, state_mlstm_C, state_mlstm_n, state_mlstm_m, state_ret,
           g_norm1, w_in, b_gates, g_mlstm_norm, g_ret_norm, w_out, g_norm2,
           w_gate, w_up, w_down, g_final):
    TPRE, TP, NS = 2048, 2048, 2
    f = lambda a: np.ascontiguousarray(np.asarray(a, dtype=np.float32))
    x_prompt, x_sample = f(x_prompt), f(x_sample)
    shared = dict(w_in=f(w_in)[0], w_out=f(w_out)[0], w_gate=f(w_gate)[0], w_up=f(w_up)[0], w_down=f(w_down)[0])
    sC, sn, sm, sR = f(state_mlstm_C)[0], f(state_mlstm_n)[0], f(state_mlstm_m)[0], f(state_ret)[0]
    per_core = []
    for c in range(8):
        pb, half = c // 2, c % 2
        ss = [2 * c, 2 * c + 1]
        cf, cosT, sinT = _consts(core_positions(TPRE, TP, NS, half))
        gall = make_gall(g_norm1, g_mlstm_norm, g_ret_norm, g_norm2, g_final, b_gates, float(half))
        x = np.concatenate([x_prompt[pb, 0:TPRE], x_prompt[pb, half * TP:(half + 1) * TP]] +
                           [x_sample[s] for s in ss], axis=0)
        zC = np.zeros((1, NH, HD, HD), np.float32)
        stC = np.concatenate([zC, sC[ss]], axis=0)
        stR = np.concatenate([zC, sR[ss]], axis=0)
        stn = np.concatenate([np.zeros((1, NH, HD), np.float32), sn[ss]], axis=0)
        stm = np.concatenate([np.zeros((1, NH), np.float32), sm[ss]], axis=0)
        stm = np.broadcast_to(stm[:, None, :], (stm.shape[0], 128, NH))
        d = dict(shared)
        d.update(cf=cf, gall=gall, cosT=cosT, sinT=sinT,
                 x=np.ascontiguousarray(x), stC=np.ascontiguousarray(stC), stn=np.ascontiguousarray(stn),
                 stm=np.ascontiguousarray(stm), stR=np.ascontiguousarray(stR))
        per_core.append(d)
    key = (TPRE, TP, NS)
    if key not in _CACHE:
        _CACHE[key] = build_program(TPRE, TP, NS)
    nc = _CACHE[key][0]
    res = run_bass_kernel_spmd(nc, per_core, core_ids=list(range(8))).results
    y_p = np.stack([np.concatenate([res[2 * b]["y"][:TP], res[2 * b + 1]["y"][:TP]], axis=0) for b in range(4)], axis=0)
    y_s = np.stack([res[s // 2]["y"][TP + (s % 2) * 64: TP + (s % 2 + 1) * 64] for s in range(16)], axis=0)
    C_p = np.stack([res[2 * b + 1]["oC"][0] for b in range(4)], axis=0)[None]
    n_p = np.stack([res[2 * b + 1]["on"][0] for b in range(4)], axis=0)[None]
    m_p = np.stack([res[2 * b + 1]["om"][0] for b in range(4)], axis=0)[None]
    R_p = np.stack([res[2 * b + 1]["oR"][0] for b in range(4)], axis=0)[None]
    C_s = np.stack([res[s // 2]["oC"][1 + s % 2] for s in range(16)], axis=0)[None]
    n_s = np.stack([res[s // 2]["on"][1 + s % 2] for s in range(16)], axis=0)[None]
    m_s = np.stack([res[s // 2]["om"][1 + s % 2] for s in range(16)], axis=0)[None]
    R_s = np.stack([res[s // 2]["oR"][1 + s % 2] for s in range(16)], axis=0)[None]
    return (y_p, y_s, C_p, n_p, m_p, R_p, C_s, n_s, m_s, R_s)
```
